# Optimizing a Trainium2 kernel written in Bass

```python
import math
import jax, jax.numpy as jnp
from jax import lax
import numpy as np


D_MODEL = 1024
BATCH = 2
SEQ = 8192
DEPTH = 4

N_HEADS = 8
KV_HEADS = 2
GROUP = N_HEADS // KV_HEADS
HEAD_DIM = 64
Q_DIM = N_HEADS * HEAD_DIM
KV_DIM = KV_HEADS * HEAD_DIM
WINDOW = 128
BLOCK = 128
NUM_BUCKETS = 32
MAX_DISTANCE = 128
CONV_DIM = 512
CONV_WIDTH = 31
N_BRANCHES = 2
GATE_DIM = N_BRANCHES * D_MODEL
IN_DIM = Q_DIM + 2 * KV_DIM + 2 * CONV_DIM + GATE_DIM
D_FF = 2816
PLE_DIM = 256
NEG_INF = -1e9

kernel_name = "hybrid_gated_swa_conformer_encoder"


def rms_norm(x, g, eps=1e-6):
    xf = x.astype(jnp.float32)
    y = xf * lax.rsqrt(jnp.mean(xf * xf, axis=-1, keepdims=True) + eps)
    return (y * g.astype(jnp.float32)).astype(x.dtype)


def layer_norm(x, g, b, eps=1e-5):
    xf = x.astype(jnp.float32)
    mu = jnp.mean(xf, axis=-1, keepdims=True)
    var = jnp.mean(jnp.square(xf - mu), axis=-1, keepdims=True)
    y = (xf - mu) * lax.rsqrt(var + eps)
    return (y * g.astype(jnp.float32) + b.astype(jnp.float32)).astype(x.dtype)


def swiglu(x, w_in, w_out):
    gate, up = jnp.split(x @ w_in, 2, axis=-1)
    return (jax.nn.silu(gate) * up) @ w_out


def t5_buckets(rel):
    half = NUM_BUCKETS // 2
    max_exact = half // 2
    n = jnp.abs(rel)
    ret = jnp.where(rel > 0, half, 0)
    nf = jnp.maximum(n, 1).astype(jnp.float32)
    large = max_exact + (jnp.log(nf / max_exact) / math.log(MAX_DISTANCE / max_exact)
                         * (half - max_exact)).astype(jnp.int32)
    large = jnp.minimum(large, half - 1)
    return ret + jnp.where(n < max_exact, n, large)


def window_attention(q, k, v, sink, rel_bias):
    B, S = q.shape[0], q.shape[1]
    nb = S // BLOCK
    qb = q.reshape(B, nb, BLOCK, KV_HEADS, GROUP, HEAD_DIM)

    def windows(t):
        tp = jnp.pad(t, ((0, 0), (BLOCK, BLOCK), (0, 0), (0, 0)))
        tp = tp.reshape(B, nb + 2, BLOCK, KV_HEADS, HEAD_DIM)
        return jnp.concatenate([tp[:, :-2], tp[:, 1:-1], tp[:, 2:]], axis=2)

    kw, vw = windows(k), windows(v)
    scale = HEAD_DIM ** -0.5
    s = jnp.einsum('bnqkgd,bnskd->bnkgqs', qb, kw).astype(jnp.float32) * scale

    qi = jnp.arange(BLOCK)
    kj = jnp.arange(3 * BLOCK)
    rel = kj[None, :] - BLOCK - qi[:, None]
    bias = rel_bias.astype(jnp.float32)[t5_buckets(rel)]
    bias = bias.transpose(2, 0, 1).reshape(KV_HEADS, GROUP, BLOCK, 3 * BLOCK)
    kpos = jnp.arange(nb)[:, None] * BLOCK - BLOCK + kj[None, :]
    valid = (jnp.abs(rel) <= WINDOW)[None] & ((kpos >= 0) & (kpos < S))[:, None, :]

    s = s + bias[None, None]
    s = jnp.where(valid[None, :, None, None], s, NEG_INF)
    sink_l = jnp.broadcast_to(sink.astype(jnp.float32).reshape(KV_HEADS, GROUP)[None, None, :, :, None, None],
                              s.shape[:-1] + (1,))
    pr = jax.nn.softmax(jnp.concatenate([s, sink_l], axis=-1), axis=-1)[..., :-1]
    o = jnp.einsum('bnkgqs,bnskd->bnqkgd', pr.astype(v.dtype), vw)
    return o.reshape(B, S, Q_DIM)


def depthwise_conv(c, w, b):
    pad = CONV_WIDTH // 2
    y = lax.conv_general_dilated(c, w.reshape(CONV_WIDTH, 1, CONV_DIM).astype(c.dtype),
                                 window_strides=(1,), padding=[(pad, pad)],
                                 dimension_numbers=('NWC', 'WIO', 'NWC'),
                                 feature_group_count=CONV_DIM)
    return y + b


def setup_inputs(seed: int = 0) -> dict:
    key = jax.random.key(seed)
    ks = jax.random.split(key, 26)
    f32 = jnp.float32

    def w(k, shape, fan_in):
        return jax.random.normal(k, shape, f32) * (fan_in ** -0.5)

    def gain(k, shape):
        return 1.0 + 0.02 * jax.random.normal(k, shape, f32)

    L, D = DEPTH, D_MODEL
    return {
        "x": jax.random.normal(ks[0], (BATCH, SEQ, D), f32),
        "p": jax.random.normal(ks[1], (DEPTH, BATCH, SEQ, PLE_DIM), f32),
        "rel_bias": 0.1 * jax.random.normal(ks[2], (NUM_BUCKETS, N_HEADS), f32),
        "norm_ffn1": gain(ks[3], (L, D)),
        "w_ffn1_in": w(ks[4], (L, D, 2 * D_FF), D),
        "w_ffn1_out": w(ks[5], (L, D_FF, D), D_FF),
        "norm_mix": gain(ks[6], (L, D)),
        "w_in": w(ks[7], (L, D, IN_DIM), D),
        "q_norm": gain(ks[8], (L, HEAD_DIM)),
        "k_norm": gain(ks[9], (L, HEAD_DIM)),
        "sink": 0.5 * jax.random.normal(ks[10], (L, N_HEADS), f32),
        "conv_w": w(ks[11], (L, CONV_WIDTH, CONV_DIM), CONV_WIDTH),
        "conv_b": 0.02 * jax.random.normal(ks[12], (L, CONV_DIM), f32),
        "conv_ln_g": gain(ks[13], (L, CONV_DIM)),
        "conv_ln_b": 0.02 * jax.random.normal(ks[14], (L, CONV_DIM), f32),
        "w_attn_out": w(ks[15], (L, Q_DIM, D), Q_DIM),
        "w_conv_out": w(ks[16], (L, CONV_DIM, D), CONV_DIM),
        "w_o": w(ks[17], (L, D, D), D),
        "norm_ffn2": gain(ks[18], (L, D)),
        "w_ffn2_in": w(ks[19], (L, D, 2 * D_FF), D),
        "w_ffn2_out": w(ks[20], (L, D_FF, D), D_FF),
        "norm_pe": gain(ks[21], (L, D)),
        "w_pe_gate": w(ks[22], (L, D, D), D),
        "w_pe_proj": w(ks[23], (L, PLE_DIM, D), PLE_DIM),
    }


def reference(x, p, rel_bias, norm_ffn1, w_ffn1_in, w_ffn1_out, norm_mix, w_in, q_norm, k_norm,
              sink, conv_w, conv_b, conv_ln_g, conv_ln_b, w_attn_out, w_conv_out, w_o,
              norm_ffn2, w_ffn2_in, w_ffn2_out, norm_pe, w_pe_gate, w_pe_proj):
    B, S = x.shape[0], x.shape[1]
    splits = [Q_DIM, Q_DIM + KV_DIM, Q_DIM + 2 * KV_DIM, Q_DIM + 2 * KV_DIM + 2 * CONV_DIM]
    for i in range(DEPTH):
        h = x + 0.5 * swiglu(rms_norm(x, norm_ffn1[i]), w_ffn1_in[i], w_ffn1_out[i])

        u = rms_norm(h, norm_mix[i])
        q, k, v, c, g = jnp.split(u @ w_in[i], splits, axis=-1)

        q = rms_norm(q.reshape(B, S, N_HEADS, HEAD_DIM), q_norm[i])
        k = rms_norm(k.reshape(B, S, KV_HEADS, HEAD_DIM), k_norm[i])
        v = v.reshape(B, S, KV_HEADS, HEAD_DIM)
        y_attn = window_attention(q, k, v, sink[i], rel_bias) @ w_attn_out[i]

        c_val, c_gate = jnp.split(c, 2, axis=-1)
        c = c_val * jax.nn.sigmoid(c_gate)
        c = depthwise_conv(c, conv_w[i], conv_b[i])
        c = jax.nn.silu(layer_norm(c, conv_ln_g[i], conv_ln_b[i]))
        y_conv = c @ w_conv_out[i]

        g_attn, g_conv = jnp.split(jax.nn.sigmoid(g), 2, axis=-1)
        h = h + (g_attn * y_attn + g_conv * y_conv) @ w_o[i]

        h = h + 0.5 * swiglu(rms_norm(h, norm_ffn2[i]), w_ffn2_in[i], w_ffn2_out[i])

        x = h + (p[i] @ w_pe_proj[i]) * jax.nn.sigmoid(rms_norm(h, norm_pe[i]) @ w_pe_gate[i])
    return x
```

```python
import contextlib
import numpy as np
import concourse.bass as bass
import concourse.mybir as mybir
from concourse.bass_utils import run_bass_kernel_spmd

F32, BF16 = mybir.dt.float32, mybir.dt.bfloat16
AF = mybir.ActivationFunctionType
ALU = mybir.AluOpType

D = 1024
DFF = 2816
INDIM = 3840
PLE = 256
CWID = 31
NLAYER = 4
SEQ = 8192
WIN = 2816

G1, GM, G2, GP, GQ, GK, CB, LG, LB, CWB = 0, 32, 64, 96, 128, 132, 136, 152, 168, 184
IDB = CWB + 16 * CWID
SELB = IDB + 128
NV = SELB + 128

WNAMES = dict(f1i="w_ffn1_in", f1o="w_ffn1_out", win="w_in", ao="w_attn_out", co="w_conv_out",
              wo="w_o", f2i="w_ffn2_in", f2o="w_ffn2_out", pg="w_pe_gate", pp="w_pe_proj")
WSHAPES = dict(f1i=(D, 2 * DFF), f1o=(DFF, D), win=(D, INDIM), ao=(512, D), co=(512, D),
               wo=(D, D), f2i=(D, 2 * DFF), f2o=(DFF, D), pg=(D, D), pp=(PLE, D))
NPIECE = dict(f1i=32, f1o=16, win=16, ao=4, co=4, wo=8, f2i=32, f2o=16, pg=8, pp=2)
WORDER = ["f1i", "f1o", "win", "ao", "co", "wo", "f2i", "f2o", "pg", "pp"]
ENGS = ("pe", "act", "dve", "pool", "sp")
NS = 3


class Prog:
    def __init__(self):
        self.groups = {e: [] for e in ENGS}
        self.cnt = {e: 0 for e in ENGS}
        self.water = {e: {} for e in ENGS}
        self.buf = {}
        self.dsem = {}
        self.tag = ""
        self.tags = {e: [] for e in ENGS}

    def _deps(self, eng, reads, writes):
        deps = {}

        def add(k, v):
            if eng == "pe" and k == "pe":
                return
            if deps.get(k, 0) < v:
                deps[k] = v
        for r in reads:
            st = self.buf.get(r)
            if st and st[0]:
                add(*st[0])
        for w in writes:
            st = self.buf.get(w)
            if st:
                if st[0]:
                    add(*st[0])
                for k, v in st[1].items():
                    add(k, v)
        wm = self.water[eng]
        waits = []
        for k, v in deps.items():
            if wm.get(k, 0) < v:
                wm[k] = v
                waits.append((k, v))
        return waits

    def _mark(self, tok, reads, writes):
        for r in reads:
            st = self.buf.setdefault(r, [None, {}])
            if st[1].get(tok[0], 0) < tok[1]:
                st[1][tok[0]] = tok[1]
        for w in writes:
            self.buf[w] = [tok, {}]

    def op(self, eng, fns, reads=(), writes=()):
        waits = self._deps(eng, reads, writes)
        self.cnt[eng] += 1
        tok = (eng, self.cnt[eng])
        self.groups[eng].append(("op", waits, fns))
        self.tags[eng].append((self.tag, len(fns)))
        self._mark(tok, reads, writes)

    def dma(self, eng, sem, out, in_, reads=(), writes=(), mark=True, after=()):
        waits = self._deps(eng, reads, writes)
        for k, v in after:
            if self.water[eng].get(k, 0) < v:
                self.water[eng][k] = v
                waits.append((k, v))
        self.dsem[sem] = self.dsem.get(sem, 0) + 16
        tok = (sem, self.dsem[sem])
        self.groups[eng].append(("dma", waits, (out, in_, sem)))
        if mark:
            self._mark(tok, reads, writes)
        else:
            self._mark(tok, reads, ())
        return tok

    def setw(self, keys, tok):
        for k in keys:
            self.buf[k] = [tok, {}]

    def final_wait(self, eng, sem):
        self.groups[eng].append(("wait", [(sem, self.dsem[sem])], None))


def MM(out, lhsT, rhs, start, stop):
    return lambda e: e.matmul(out, lhsT=lhsT, rhs=rhs, start=start, stop=stop)


def ACT(out, in_, func, **kw):
    return lambda e: e.activation(out=out, in_=in_, func=func, **kw)


def TT(out, a, b, op):
    return lambda e: e.tensor_tensor(out=out, in0=a, in1=b, op=op)


def STT(out, a, s, b, op0, op1):
    return lambda e: e.scalar_tensor_tensor(out=out, in0=a, scalar=s, in1=b, op0=op0, op1=op1)


def TS(out, a, s1, s2, op0, op1):
    return lambda e: e.tensor_scalar(out=out, in0=a, scalar1=s1, scalar2=s2, op0=op0, op1=op1)


def CP(out, a):
    return lambda e: e.tensor_copy(out=out, in_=a)


def RCP(out, a):
    return lambda e: e.reciprocal(out=out, in_=a)


def ACOPY(out, in_):
    return lambda e: e.copy(out=out, in_=in_)


def MSET(out, v):
    return lambda e: e.memset(out, v)


def build(NTOK, NL, stop=None):
    TW = [512] * (NTOK // 512) + ([NTOK % 512] if NTOK % 512 else [])
    if NTOK % 512 == 256 and NTOK >= 768:
        TW = [512] * (NTOK // 512 - 1) + [384, 384]
    T0 = [sum(TW[:i]) for i in range(len(TW))]
    NT = len(TW)
    NB = NTOK // 128
    assert NTOK % 128 == 0
    nc = bass.Bass("TRN2", target_bir_lowering=False)
    P = Prog()

    def din(name, shape):
        return nc.dram_tensor(name, list(shape), F32, kind="ExternalInput").ap()
    xT = din("xT", (D, NTOK))
    pT = din("pT", (NL, PLE, NTOK))
    vecs_d = din("vecs", (128, NV))
    bias_d = din("biasT", (128, 3072))
    sink_d = din("sinkr", (2, NL * 512))
    Wd = {m: din(WNAMES[m], (NL,) + WSHAPES[m]) for m in WORDER}
    Wb = {m: nc.dram_tensor("wb_" + m, [NL] + list(WSHAPES[m]), BF16, kind="Internal").ap() for m in WORDER}
    outT = nc.dram_tensor("outT", [D, NTOK], F32, kind="ExternalOutput").ap()

    es = contextlib.ExitStack()
    with es:
        def sb(name, shape, dt):
            return es.enter_context(nc.sbuf_tensor(name, list(shape), dt))
        xres = sb("xres", (128, 8, NTOK), F32)
        kT = sb("kT", (128, NTOK), BF16)
        vtok = sb("vtok", (128, NB, 128), BF16)
        vpad = sb("vpad", (128, 2, 6, 128), BF16)
        unedge = sb("unedge", (128, NT, 2, 8, 16), BF16)
        un = sb("un", (128, 8, 512), BF16)
        wkt = sb("wkt", (128, 26 * 512), BF16)
        cgx = sb("cgx", (128, 4, 544), BF16)
        biasb = sb("biasb", (128, 3072), BF16)
        vecs = sb("vecs_s", (128, NV), F32)
        gq8 = sb("gq8", (128, NL), F32)
        identb = sb("identb", (128, 128), BF16)
        sel2b = sb("sel2b", (2, 128), BF16)
        onesD = sb("onesD", (128, 128), BF16)
        ones512 = sb("ones512", (128, 128), BF16)
        bd64 = sb("bd64", (128, 128), BF16)
        onespad = sb("onespad", (128, 2, 128), BF16)
        sinks = sb("sinks", (2, NL * 512), F32)
        esrow = sb("esrow", (2, NL * 512), BF16)
        xsq = [sb(f"xsq{i}", (128, 512), BF16) for i in range(2)]
        rs = sb("rs", (128, 512), F32)
        TMP = [sb(f"tmp{i}", (128, 512), F32) for i in range(4)]
        acc = [TMP[2], TMP[3]]
        sgh = sb("sgh", (128, 32), F32)
        pTb = [sb(f"pTb{i}", (128, 2, 512), BF16) for i in range(1)]
        ringt = [sb(f"ring{i}", (128, 4096), BF16) for i in range(NS)]
        ps = [es.enter_context(nc.psum_tensor(f"ps{i}", [128, 512], F32)) for i in range(8)]

        state = dict(ps=0, ring=0, ptb=0, nload=0, w=512, ub="A", layer=0, light=False, sweepB=False)
        pending = []

        def nextps():
            i = state["ps"]
            state["ps"] = (i + 1) % 8
            return i

        def wk(i):
            return wkt[:, i * 512:i * 512 + state["w"]]

        def ring_load(parts):
            i = state["ring"] % NS
            state["ring"] += 1
            keys = []
            for (_, _, sk) in parts:
                _, m_, l_ = sk.split(":")
                idx = [n for n, pc in enumerate(pending) if pc[0] == m_ and pc[1] == int(l_)]
                if idx:
                    for _ in range(idx[-1] + 1):
                        issue_cast(pending.pop(0))
            tok = None
            for pi, (dstf, src, sk) in enumerate(parts):
                key = f"wr{i}_{pi}"
                tok = P.dma("sp", key, dstf(ringt[i]), src, reads=[sk], writes=[key])
                keys.append(key)
            state["nload"] += 1
            if pending:
                own = pending[0][1] == state["layer"]
                if own or state["light"] or state["nload"] % 3 == 0:
                    issue_cast(pending.pop(0), after=[tok])
            return ringt[i], keys

        def v8(lo, n):
            return lambda s: s[:, lo:lo + 8 * n].rearrange("p (k n) -> p k n", k=8)

        P.dma("sp", "vec", vecs[:], vecs_d[:, :], writes=["vecs"])
        P.dma("sp", "snk", sinks[:], sink_d[:, :], writes=["sinks"])
        P.dma("pool", "bia", biasb[:], bias_d[:, :], writes=["biasb"])
        xTv = xT.rearrange("(c p) n -> p c n", p=128)
        for t in range(NT):
            P.dma("sp", f"xin{t}", xres[:, :, T0[t]:T0[t] + TW[t]], xTv[:, :, T0[t]:T0[t] + TW[t]],
                  writes=[f"x:{t}:{c}" for c in range(8)])

        def cast_layer_pieces(l):
            lst = []
            for m in WORDER:
                rows = WSHAPES[m][0]
                npc = NPIECE[m]
                r = rows // npc
                for i in range(npc):
                    lst.append((m, l, i, r))
            return lst

        def issue_cast(piece, after=()):
            m, l, i, r = piece
            sem = f"c_{m}_{l}"
            if m == "win":
                for a_ in range(2):
                    P.dma("pool", sem,
                          Wb[m][l, i * r:(i + 1) * r, 0:512].rearrange("r (j a n) -> r a j n", j=4, a=2)[:, a_],
                          Wd[m][l, i * r:(i + 1) * r, a_ * 256:(a_ + 1) * 256].rearrange("r (j n) -> r j n", j=4),
                          mark=False, after=after)
                tok = P.dma("pool", sem, Wb[m][l, i * r:(i + 1) * r, 512:INDIM], Wd[m][l, i * r:(i + 1) * r, 512:INDIM],
                            mark=False, after=after)
            else:
                tok = P.dma("pool", sem, Wb[m][l, i * r:(i + 1) * r, :], Wd[m][l, i * r:(i + 1) * r, :], mark=False,
                            after=after)
            if i == NPIECE[m] - 1:
                P.setw([f"wb:{m}:{l}"], tok)

        for pc in cast_layer_pieces(0):
            if pc[0] == "f1i":
                issue_cast(pc)
            else:
                pending.append(pc)

        P.op("dve", [MSET(onesD[:], 1.0 / 1024), MSET(ones512[:], 1.0 / 512), MSET(bd64[:], 0.0),
                     MSET(onespad[:], 0.0), MSET(vpad[:], 0.0)], writes=["consts"])
        P.op("dve", [MSET(bd64[0:64, 0:64], 1.0 / 64), MSET(bd64[64:128, 64:128], 1.0 / 64),
                     MSET(onespad[:, 0, 0:64], 1.0), MSET(onespad[:, 1, 64:128], 1.0)],
             reads=["consts"], writes=["consts"])
        P.op("dve", [CP(identb[:], vecs[:, IDB:IDB + 128]), CP(sel2b[:], vecs[0:2, SELB:SELB + 128]),
                     (lambda e: e.tensor_scalar_mul(out=gq8[:], in0=vecs[:, GQ:GQ + NL], scalar1=0.125))],
             reads=["vecs", "consts"], writes=["consts"])
        P.op("act", [ACT(esrow[:], sinks[:], AF.Exp)], reads=["sinks"], writes=["esrow"])
        CONST = ["consts"]

        def tokslice(t):
            state["w"] = TW[t]
            return slice(T0[t], T0[t] + TW[t])

        def W():
            return state["w"]

        def psv(i):
            return ps[i][:, :state["w"]]

        def unv(c):
            if state["ub"] == "A":
                return un[:, c, :state["w"]]
            o = (c % 2) * 512
            return TMP[c // 2][:].bitcast(BF16)[:, o:o + state["w"]]

        def unkey(c):
            return f"un:{c}" if state["ub"] == "A" else f"tmp{c // 2}"

        def unk():
            return sorted(set(unkey(c) for c in range(8)))

        def norm(l, t, gbase):
            tok = tokslice(t)
            pi = nextps()
            for c in range(8):
                sq = xsq[c % 2]
                P.op("act", [ACT(sq[:, :W()], xres[:, c, tok], AF.Square)], reads=[f"x:{t}:{c}"], writes=[f"xsq{c % 2}"])
                P.op("pe", [MM(psv(pi), onesD[:], sq[:, :W()], c == 0, c == 7)], reads=[f"xsq{c % 2}"] + CONST,
                     writes=[f"ps{pi}"])
            P.op("act", [ACT(rs[:, :W()], psv(pi), AF.Ln, bias=1e-6, scale=1.0)], reads=[f"ps{pi}"], writes=["rs"])
            P.op("act", [ACT(rs[:, :W()], rs[:, :W()], AF.Exp, scale=-0.5)], reads=["rs"], writes=["rs"])
            for c in range(8):
                g = vecs[:, gbase + l * 8 + c: gbase + l * 8 + c + 1]
                P.op("dve", [STT(unv(c), xres[:, c, tok], g, rs[:, :W()], ALU.mult, ALU.mult)],
                     reads=[f"x:{t}:{c}", "rs", "vecs"], writes=[unkey(c)])

        def ffn(l, t, m_in, m_out, gbase):
            ffn_norm(l, t, gbase)
            ffn_in(l, t, m_in, range(11))
            ffn_out(l, t, m_out)

        def ffn_norm(l, t, gbase):
            P.tag = "ffn.norm"
            norm(l, t, gbase)

        def ffn_in(l, t, m_in, pairs):
            tok = tokslice(t)
            P.tag = "ffn.in"
            win = Wb[m_in][l].rearrange("(kc p) n -> p kc n", p=128)
            kin = f"wb:{m_in}:{l}"
            for i in pairs:
                slot, keys = ring_load([(v8(0, 256), win[:, :, 256 * i:256 * i + 256], kin),
                                        (v8(2048, 256), win[:, :, DFF + 256 * i:DFF + 256 * i + 256], kin)])
                for a in range(2):
                    fc = 2 * i + a
                    pg, pu = nextps(), nextps()
                    ins = []
                    for kc in range(8):
                        o = kc * 256 + a * 128
                        ins.append(MM(psv(pg), slot[:, o:o + 128], unv(kc), kc == 0, kc == 7))
                    for kc in range(8):
                        o = 2048 + kc * 256 + a * 128
                        ins.append(MM(psv(pu), slot[:, o:o + 128], unv(kc), kc == 0, kc == 7))
                    P.op("pe", ins, reads=keys + unk(), writes=[f"ps{pg}", f"ps{pu}"])
                    P.op("act", [ACT(wk(fc), psv(pg), AF.Silu)], reads=[f"ps{pg}"], writes=[f"wk:{fc}"])
                    P.op("dve", [TT(wk(fc), wk(fc), psv(pu), ALU.mult)], reads=[f"wk:{fc}", f"ps{pu}"],
                         writes=[f"wk:{fc}"])

        def ffn_out(l, t, m_out):
            tok = tokslice(t)
            P.tag = "ffn.out"
            wout = Wb[m_out][l].rearrange("(fc p) n -> p fc n", p=128)
            kout = f"wb:{m_out}:{l}"
            for i in range(4):
                po = [nextps(), nextps()]
                for hh in range(2):
                    slot, keys = ring_load([(lambda s: s[:, 0:2816].rearrange("p (f n) -> p f n", f=11),
                                             wout[:, 11 * hh:11 * hh + 11, 256 * i:256 * i + 256], kout)])
                    ins = []
                    for a in range(2):
                        for f in range(11):
                            o = f * 256 + a * 128
                            ins.append(MM(psv(po[a]), slot[:, o:o + 128], wk(11 * hh + f),
                                          hh == 0 and f == 0, hh == 1 and f == 10))
                    P.op("pe", ins, reads=keys + [f"wk:{11 * hh + f}" for f in range(11)],
                         writes=[f"ps{po[0]}", f"ps{po[1]}"])
                for a in range(2):
                    oc = 2 * i + a
                    P.op("dve", [STT(xres[:, oc, tok], psv(po[a]), 0.5, xres[:, oc, tok], ALU.mult, ALU.add)],
                         reads=[f"ps{po[a]}", f"x:{t}:{oc}"], writes=[f"x:{t}:{oc}"])

        def qknorm(pq, gcol_ap, out_ap, outkeys, blk=False):
            vw = (lambda a: a.rearrange("p (b q) -> p b q", b=W() // 128)) if blk else (lambda a: a)
            P.op("act", [ACT(xsq[0][:, :W()], psv(pq), AF.Square)], reads=[f"ps{pq}"], writes=["xsq0"])
            pst = nextps()
            P.op("pe", [MM(psv(pst), bd64[:], xsq[0][:, :W()], True, True)], reads=["xsq0"] + CONST, writes=[f"ps{pst}"])
            P.op("act", [ACT(rs[:, :W()], psv(pst), AF.Ln, bias=1e-6, scale=1.0)], reads=[f"ps{pst}"], writes=["rs"])
            P.op("act", [ACT(rs[:, :W()], rs[:, :W()], AF.Exp, scale=-0.5)], reads=["rs"], writes=["rs"])
            P.op("dve", [STT(out_ap, vw(psv(pq)), gcol_ap, vw(rs[:, :W()]), ALU.mult, ALU.mult)],
                 reads=[f"ps{pq}", "rs", "vecs"] + CONST, writes=outkeys)

        def kv_norm(l, t):
            P.tag = "kv"
            norm(l, t, GM)

        def kv_phase(l, t):
            kv_norm(l, t)
            kv_body(l, t)

        def kv_body(l, t):
            tok = tokslice(t)
            P.tag = "kv"
            P.op("pool", [CP(unedge[:, t, 0, c, :], unv(c)[:, 0:16]) for c in range(8)] +
                 [CP(unedge[:, t, 1, c, :], unv(c)[:, W() - 16:W()]) for c in range(8)],
                 reads=unk(), writes=[f"ue:{t}"])
            win = Wb["win"][l].rearrange("(kc p) n -> p kc n", p=128)
            slot, keys = ring_load([(v8(0, 256), win[:, :, 512:768], f"wb:win:{l}")])
            pk = nextps()
            P.op("pe", [MM(psv(pk), slot[:, kc * 256:kc * 256 + 128], unv(kc), kc == 0, kc == 7)
                        for kc in range(8)], reads=keys + unk(), writes=[f"ps{pk}"])
            pv = nextps()
            ins = []
            nb = W() // 128
            B0 = T0[t] // 128
            for b in range(nb):
                for kc in range(8):
                    ins.append(MM(ps[pv][:, b * 128:(b + 1) * 128], unv(kc)[:, b * 128:(b + 1) * 128],
                                  slot[:, kc * 256 + 128:kc * 256 + 256], kc == 0, kc == 7))
            P.op("pe", ins, reads=keys + unk(), writes=[f"ps{pv}"])
            qknorm(pk, vecs[:, GK + l:GK + l + 1], kT[:, tok], [f"k:{B0 + b}" for b in range(nb)])
            P.op("act", [ACOPY(vtok[:, B0:B0 + nb, :], psv(pv).rearrange("p (b n) -> p b n", b=nb))],
                 reads=[f"ps{pv}"], writes=[f"v:{B0 + b}" for b in range(nb)])

        ESL = [[0, 1, 2, 3, 20, 21], [12, 13, 14, 15, 16, 17]]
        QO = 22

        def qo_view():
            return wkt[:, QO * 512:(QO + 4) * 512].rearrange("p (j t) -> p j t", j=4)

        def mixer(l, t, prenormed=False):
            P.tag = "mix.norm"
            state["light"] = True
            if not prenormed:
                norm(l, t, GM)
            mixer_q(l, t)
            mixer_c(l, t)
            mixer_rest(l, t, lambda: None)

        def mixer_q(l, t):
            tok = tokslice(t)
            state["light"] = True
            P.tag = "mix.q"
            win = Wb["win"][l].rearrange("(kc p) n -> p kc n", p=128)
            kwin = f"wb:win:{l}"
            slot, keys = ring_load([(v8(0, 256), win[:, :, 0:256], kwin), (v8(2048, 256), win[:, :, 256:512], kwin)])
            qo4 = wkt[:, QO * 512:(QO + 4) * 512].rearrange("p (b j q) -> p b j q", b=4, j=4)
            for j in range(4):
                pq = nextps()
                o = (j // 2) * 2048 + (j % 2) * 128
                P.op("pe", [MM(psv(pq), slot[:, o + kc * 256:o + kc * 256 + 128], unv(kc), kc == 0, kc == 7)
                            for kc in range(8)], reads=keys + unk(), writes=[f"ps{pq}"])
                qknorm(pq, gq8[:, l:l + 1], qo4[:, :W() // 128, j, :], [f"qo:{j}:{b}" for b in range(W() // 128)], blk=True)

        def mixer_c(l, t):
            tok = tokslice(t)
            state["light"] = True
            win = Wb["win"][l].rearrange("(kc p) n -> p kc n", p=128)
            kwin = f"wb:win:{l}"
            sgs = wkt[:, 4 * 512:6 * 512].bitcast(F32)[:, :W()]
            sgk = ["wk:4", "wk:5"]
            P.tag = "mix.cproj"
            if t == 0:
                P.op("pool", [MSET(cgx[:, :, 0:16], 0.0)], writes=[f"cg:{ch}" for ch in range(4)])
            if t == NT - 1:
                P.op("pool", [MSET(cgx[:, :, 16 + W():32 + W()], 0.0)], writes=[f"cg:{ch}" for ch in range(4)])
            sv_, kv_ = ring_load([(v8(0, 256), win[:, :, 768:1024], kwin), (v8(2048, 256), win[:, :, 1024:1280], kwin)])
            sg_, kg_ = ring_load([(v8(0, 256), win[:, :, 1280:1536], kwin), (v8(2048, 256), win[:, :, 1536:1792], kwin)])
            for ch in range(4):
                o = (ch // 2) * 2048 + (ch % 2) * 128
                pval, pgate, ph = nextps(), nextps(), nextps()
                ins = []
                for kc in range(8):
                    ins.append(MM(psv(pval), sv_[:, o + kc * 256:o + kc * 256 + 128], unv(kc), kc == 0, kc == 7))
                for kc in range(8):
                    ins.append(MM(psv(pgate), sg_[:, o + kc * 256:o + kc * 256 + 128], unv(kc), kc == 0, kc == 7))
                halos = []
                if t > 0:
                    halos.append((0, unedge[:, t - 1, 1, :, :], f"ue:{t - 1}"))
                if t < NT - 1:
                    halos.append((1, unedge[:, t + 1, 0, :, :], f"ue:{t + 1}"))
                for side, ue, uk in halos:
                    for kc in range(8):
                        ins.append(MM(ps[ph][:, side * 16:side * 16 + 16], sv_[:, o + kc * 256:o + kc * 256 + 128],
                                      ue[:, kc, :], kc == 0, kc == 7))
                    for kc in range(8):
                        ins.append(MM(ps[ph][:, 32 + side * 16:32 + side * 16 + 16],
                                      sg_[:, o + kc * 256:o + kc * 256 + 128], ue[:, kc, :], kc == 0, kc == 7))
                P.op("pe", ins, reads=kv_ + kg_ + unk() + [h[2] for h in halos],
                     writes=[f"ps{pval}", f"ps{pgate}", f"ps{ph}"])
                fa = [ACT(sgs, psv(pgate), AF.Sigmoid)]
                fd = [TT(cgx[:, ch, 16:16 + W()], psv(pval), sgs, ALU.mult)]
                for side, ue, uk in halos:
                    fa.append(ACT(sgh[:, side * 16:16 + side * 16], ps[ph][:, 32 + side * 16:48 + side * 16], AF.Sigmoid))
                    dst = cgx[:, ch, 0:16] if side == 0 else cgx[:, ch, 16 + W():32 + W()]
                    fd.append(TT(dst, ps[ph][:, side * 16:side * 16 + 16], sgh[:, side * 16:16 + side * 16], ALU.mult))
                P.op("act", fa, reads=[f"ps{pgate}", f"ps{ph}"], writes=sgk + ["sgh"])
                P.op("dve", fd, reads=sgk + ["sgh", f"ps{pval}", f"ps{ph}"], writes=[f"cg:{ch}"])

        def mixer_rest(l, t, after_merge):
            tok = tokslice(t)
            state["light"] = True
            win = Wb["win"][l].rearrange("(kc p) n -> p kc n", p=128)
            kwin = f"wb:win:{l}"
            P.tag = "mix.attn"
            oT = wkt[:, 4 * 512:8 * 512].rearrange("p (j t) -> p j t", j=4)
            wtile = W()
            state["w"] = 512
            nblk = wtile // 128

            def att_S(b):
                B = T0[t] // 128 + b
                st_ = B % 2
                kbs = [kb for kb in range(3) if 0 <= B + kb - 1 < NB]
                cps = []
                for kb in kbs:
                    cps.append(CP(vpad[:, st_, kb * 2 + 0, 0:64], vtok[:, B + kb - 1, 0:64]))
                    cps.append(CP(vpad[:, st_, kb * 2 + 1, 64:128], vtok[:, B + kb - 1, 64:128]))
                P.op("pool", cps, reads=[f"v:{B + kb - 1}" for kb in kbs] + CONST, writes=[f"vp:{st_}"])
                slist = []
                for kh in range(2):
                    for kb in kbs:
                        si = nextps()
                        KB = B + kb - 1
                        P.op("pe", [MM(psv(si), kT[kh * 64:(kh + 1) * 64, KB * 128:(KB + 1) * 128],
                                       wkt[kh * 64:(kh + 1) * 64, (QO + b) * 512:(QO + b + 1) * 512], True, False),
                                    MM(psv(si), identb[:], biasb[:, (kb * 2 + kh) * 512:(kb * 2 + kh + 1) * 512], False, True)],
                             reads=[f"k:{KB}", "biasb"] + [f"qo:{j}:{b}" for j in range(4)] + CONST, writes=[f"ps{si}"])
                        slist.append((kh, kb, ESL[b % 2][kh * 3 + kb], si))
                return slist

            def att_E(slist):
                for kh, kb, ei, si in slist:
                    P.op("act", [ACT(wk(ei), psv(si), AF.Exp)], reads=[f"ps{si}"], writes=[f"wk:{ei}"])

            def att_PV(b, elist):
                B = T0[t] // 128 + b
                st_ = B % 2
                po, pd = nextps(), nextps()
                ins = []
                for n, (kh, kb, ei, si) in enumerate(elist):
                    ins.append(MM(psv(po), vpad[:, st_, kb * 2 + kh, :], wk(ei), n == 0, n == len(elist) - 1))
                for n, (kh, kb, ei, si) in enumerate(elist):
                    ins.append(MM(psv(pd), onespad[:, kh, :], wk(ei), n == 0, False))
                ins.append(MM(psv(pd), sel2b[0:2, :], esrow[0:2, l * 512:(l + 1) * 512], False, True))
                P.op("pe", ins, reads=[f"wk:{e[2]}" for e in elist] + [f"vp:{st_}", "esrow"] + CONST,
                     writes=[f"ps{po}", f"ps{pd}"])
                P.op("act", [ACT(TMP[2][:, :W()], psv(pd), AF.Ln)], reads=[f"ps{pd}"], writes=["tmp2"])
                P.op("act", [ACT(TMP[2][:, :W()], TMP[2][:, :W()], AF.Exp, scale=-1.0)], reads=["tmp2"], writes=["tmp2"])
                P.op("dve", [TT(oT[:, :, b * 128:(b + 1) * 128], psv(po).rearrange("p (j q) -> p j q", j=4),
                                TMP[2][:, :W()].rearrange("p (j q) -> p j q", j=4), ALU.mult)],
                     reads=[f"ps{po}", "tmp2"] + [f"wk:{4 + j}" for j in range(4)], writes=[f"wk:{4 + j}" for j in range(4)])

            cur = att_S(0)
            att_E(cur)
            state["w"] = wtile
            P.tag = "mix.conv"
            dgs = [wkt[:, 12 * 512:16 * 512].rearrange("p (k n) -> p k n", k=16),
                   wkt[:, 16 * 512:20 * 512].rearrange("p (k n) -> p k n", k=16)]
            dkeys = [[f"wk:{c}" for c in range(12, 16)], [f"wk:{c}" for c in range(16, 20)]]
            for ch in range(4):
                pc = nextps()
                for hf in range(2):
                    taps = list(range(16)) if hf == 0 else list(range(16, CWID))
                    bld = []
                    for k, tap in enumerate(taps):
                        wcol = CWB + (l * 4 + ch) * CWID + tap
                        bld.append((lambda o, w_: (lambda e: e.tensor_scalar_mul(out=o, in0=identb[:], scalar1=w_)))(
                            dgs[hf][:, k, :], vecs[:, wcol:wcol + 1]))
                    P.op("dve", bld, reads=["vecs"] + CONST + dkeys[hf], writes=dkeys[hf])
                    P.op("pe", [MM(psv(pc), dgs[hf][:, k, :], cgx[:, ch, tap + 1:tap + 1 + W()], tap == 0, tap == CWID - 1)
                                for k, tap in enumerate(taps)], reads=dkeys[hf] + [f"cg:{ch}"], writes=[f"ps{pc}"])
                bcol = CB + l * 4 + ch
                P.op("act", [ACT(wk(8 + ch), psv(pc), AF.Identity, bias=vecs[:, bcol:bcol + 1], scale=1.0)],
                     reads=[f"ps{pc}", "vecs"], writes=[f"wk:{8 + ch}"])
            P.tag = "mix.attn"
            state["w"] = 512
            for b in range(nblk):
                nxt = att_S(b + 1) if b + 1 < nblk else None
                att_PV(b, cur)
                if nxt is not None:
                    att_E(nxt)
                cur = nxt
            P.tag = "mix.merge"
            state["w"] = wtile
            P.tag = "mix.ln"
            pm, pe2 = nextps(), nextps()
            for ch in range(4):
                P.op("pe", [MM(psv(pm), ones512[:], wk(8 + ch), ch == 0, ch == 3)], reads=[f"wk:{8 + ch}"] + CONST,
                     writes=[f"ps{pm}"])
                P.op("act", [ACT(xsq[ch % 2][:, :W()], wk(8 + ch), AF.Square)], reads=[f"wk:{8 + ch}"], writes=[f"xsq{ch % 2}"])
                P.op("pe", [MM(psv(pe2), ones512[:], xsq[ch % 2][:, :W()], ch == 0, ch == 3)], reads=[f"xsq{ch % 2}"] + CONST,
                     writes=[f"ps{pe2}"])
            mu, var = TMP[0], TMP[1]
            P.op("act", [ACOPY(mu[:, :W()], psv(pm))], reads=[f"ps{pm}"], writes=["tmp0"])
            P.op("dve", [TT(var[:, :W()], mu[:, :W()], mu[:, :W()], ALU.mult)], reads=["tmp0"], writes=["tmp1"])
            P.op("dve", [TT(var[:, :W()], psv(pe2), var[:, :W()], ALU.subtract)], reads=["tmp1", f"ps{pe2}"], writes=["tmp1"])
            P.op("act", [ACT(var[:, :W()], var[:, :W()], AF.Ln, bias=1e-5, scale=1.0)], reads=["tmp1"], writes=["tmp1"])
            P.op("act", [ACT(var[:, :W()], var[:, :W()], AF.Exp, scale=-0.5)], reads=["tmp1"], writes=["tmp1"])
            for ch in range(4):
                tc_ = TMP[2 + ch % 2]
                tk = f"tmp{2 + ch % 2}"
                P.op("dve", [TT(tc_[:, :W()], wk(8 + ch), mu[:, :W()], ALU.subtract)], reads=[f"wk:{8 + ch}", "tmp0"], writes=[tk])
                P.op("dve", [TT(tc_[:, :W()], tc_[:, :W()], var[:, :W()], ALU.mult)], reads=[tk, "tmp1"], writes=[tk])
                gcol, bcol = LG + l * 4 + ch, LB + l * 4 + ch
                P.op("act", [ACT(wk(8 + ch), tc_[:, :W()], AF.Silu, scale=vecs[:, gcol:gcol + 1], bias=vecs[:, bcol:bcol + 1])],
                     reads=[tk, "vecs"], writes=[f"wk:{8 + ch}"])
            state["w"] = wtile
            wao = Wb["ao"][l]
            wco = Wb["co"][l].rearrange("(c p) n -> p c n", p=128)
            for i in range(4):
                cs_ = slice(256 * i, 256 * i + 256)
                s2, k2 = ring_load([(v8(0, 256), win[:, :, 1792 + 256 * i:1792 + 256 * i + 256], kwin),
                                    (v8(2048, 256), win[:, :, 2816 + 256 * i:2816 + 256 * i + 256], kwin)])
                for a in range(2):
                    pga, pgc = nextps(), nextps()
                    ins = []
                    for kc in range(8):
                        o = kc * 256 + a * 128
                        ins.append(MM(psv(pga), s2[:, o:o + 128], unv(kc), kc == 0, kc == 7))
                    for kc in range(8):
                        o = 2048 + kc * 256 + a * 128
                        ins.append(MM(psv(pgc), s2[:, o:o + 128], unv(kc), kc == 0, kc == 7))
                    P.op("pe", ins, reads=k2 + unk(), writes=[f"ps{pga}", f"ps{pgc}"])
                    P.op("act", [ACT(wk(2 * a), psv(pga), AF.Sigmoid)], reads=[f"ps{pga}"], writes=[f"wk:{2 * a}"])
                    P.op("act", [ACT(wk(2 * a + 1), psv(pgc), AF.Sigmoid)], reads=[f"ps{pgc}"], writes=[f"wk:{2 * a + 1}"])
                s1, k1 = ring_load([
                    (lambda s: s[0:64, 0:1024].rearrange("p (j n) -> p j n", j=4),
                     wao[0:256, :].rearrange("(j p) n -> p j n", p=64)[:, :, cs_], f"wb:ao:{l}"),
                    (lambda s: s[64:128, 0:1024].rearrange("p (j n) -> p j n", j=4),
                     wao[256:512, :].rearrange("(j p) n -> p j n", p=64)[:, :, cs_], f"wb:ao:{l}"),
                    (lambda s: s[:, 1024:2048].rearrange("p (j n) -> p j n", j=4), wco[:, :, cs_], f"wb:co:{l}")])
                for a in range(2):
                    oc = 2 * i + a
                    pya, pyc = nextps(), nextps()
                    ins = []
                    for j in range(4):
                        o = j * 256 + a * 128
                        ins.append(MM(psv(pya), s1[:, o:o + 128], wk(4 + j), j == 0, j == 3))
                    for c in range(4):
                        o = 1024 + c * 256 + a * 128
                        ins.append(MM(psv(pyc), s1[:, o:o + 128], wk(8 + c), c == 0, c == 3))
                    P.op("pe", ins, reads=k1 + [f"wk:{4 + j}" for j in range(4)] + [f"wk:{8 + c}" for c in range(4)],
                         writes=[f"ps{pya}", f"ps{pyc}"])
                    ta, tb = TMP[0], TMP[1]
                    P.op("dve", [TT(ta[:, :W()], psv(pya), wk(2 * a), ALU.mult)], reads=[f"ps{pya}", f"wk:{2 * a}"], writes=["tmp0"])
                    P.op("dve", [TT(tb[:, :W()], psv(pyc), wk(2 * a + 1), ALU.mult)], reads=[f"ps{pyc}", f"wk:{2 * a + 1}"], writes=["tmp1"])
                    P.op("dve", [TT(wk(12 + oc), ta[:, :W()], tb[:, :W()], ALU.add)], reads=["tmp0", "tmp1"], writes=[f"wk:{12 + oc}"])
            after_merge()
            tok = tokslice(t)
            P.tag = "mix.wo"
            state["light"] = False
            wo = Wb["wo"][l].rearrange("(kc p) n -> p kc n", p=128)
            for i2 in range(2):
                slot, keys = ring_load([(v8(0, 256), wo[:, :, 512 * i2:512 * i2 + 256], f"wb:wo:{l}"),
                                        (v8(2048, 256), wo[:, :, 512 * i2 + 256:512 * i2 + 512], f"wb:wo:{l}")])
                for q4 in range(4):
                    oc = 4 * i2 + q4
                    o0 = (q4 // 2) * 2048 + (q4 % 2) * 128
                    po = nextps()
                    P.op("pe", [MM(psv(po), slot[:, o0 + kc * 256:o0 + kc * 256 + 128], wk(12 + kc), kc == 0, kc == 7)
                                for kc in range(8)], reads=keys + [f"wk:{12 + kc}" for kc in range(8)], writes=[f"ps{po}"])
                    P.op("dve", [TT(xres[:, oc, tok], psv(po), xres[:, oc, tok], ALU.add)],
                         reads=[f"ps{po}", f"x:{t}:{oc}"], writes=[f"x:{t}:{oc}"])

        def ple(l, t, after_norm=None):
            tok = tokslice(t)
            P.tag = "ple"
            state["light"] = True
            norm(l, t, GP)
            if after_norm is not None:
                ub_ = state["ub"]
                after_norm()
                state["ub"] = ub_
                tok = tokslice(t)
                P.tag = "ple"
            pi = 0
            state["ptb"] += 1
            P.dma("pool", f"pt{pi}", pTb[pi][:, :, :W()], pT[l].rearrange("(kc p) t -> p kc t", p=128)[:, :, tok],
                  writes=[f"ptb{pi}"])
            wpp = Wb["pp"][l].rearrange("(kc p) n -> p kc n", p=128)
            wpg = Wb["pg"][l].rearrange("(kc p) n -> p kc n", p=128)
            for i in range(4):
                cs_ = slice(256 * i, 256 * i + 256)
                slot, keys = ring_load([(v8(0, 256), wpg[:, :, cs_], f"wb:pg:{l}"),
                                        (lambda s: s[:, 2048:2560].rearrange("p (k n) -> p k n", k=2), wpp[:, :, cs_], f"wb:pp:{l}")])
                for a in range(2):
                    oc = 2 * i + a
                    pgt, ppe = nextps(), nextps()
                    ins = [MM(psv(pgt), slot[:, kc * 256 + a * 128:kc * 256 + a * 128 + 128], unv(kc), kc == 0, kc == 7)
                           for kc in range(8)]
                    ins += [MM(psv(ppe), slot[:, 2048 + kc * 256 + a * 128:2048 + kc * 256 + a * 128 + 128], pTb[pi][:, kc, :W()],
                               kc == 0, kc == 1) for kc in range(2)]
                    P.op("pe", ins, reads=keys + unk() + [f"ptb{pi}"], writes=[f"ps{pgt}", f"ps{ppe}"])
                    tq = wkt[:, a * 1024:(a + 1) * 1024].bitcast(F32)[:, :W()]
                    tqk = [f"wk:{2 * a}", f"wk:{2 * a + 1}"]
                    P.op("act", [ACT(tq, psv(pgt), AF.Sigmoid)], reads=[f"ps{pgt}"], writes=tqk)
                    P.op("dve", [TT(tq, psv(ppe), tq, ALU.mult)], reads=[f"ps{ppe}"] + tqk, writes=tqk)
                    P.op("dve", [TT(xres[:, oc, tok], tq, xres[:, oc, tok], ALU.add)],
                         reads=tqk + [f"x:{t}:{oc}"], writes=[f"x:{t}:{oc}"])

        def store_tile(t):
            tok = tokslice(t)
            for c in range(8):
                P.dma("sp", "out", outT[c * 128:(c + 1) * 128, tok], xres[:, c, tok], reads=[f"x:{t}:{c}"])

        done = False
        for l in range(NL):
            state["layer"] = l
            state["sweepB"] = False
            pending.extend(cast_layer_pieces(l + 1) if l + 1 < NL else [])
            if stop == "ffn1":
                for t in range(NT):
                    ffn(l, t, "f1i", "f1o", G1)
            else:
                K1 = 3
                UB = "B"
                state["ub"] = "A"
                ffn_norm(l, 0, G1)
                started = 0
                for t in range(NT):
                    state["ub"] = "A"
                    ffn_in(l, t, "f1i", range(started, 11))
                    if t + 1 < NT:
                        ffn_norm(l, t + 1, G1)
                    ffn_out(l, t, "f1o")
                    state["ub"] = UB
                    kv_norm(l, t)
                    started = 0
                    if t + 1 < NT:
                        state["ub"] = "A"
                        ffn_in(l, t + 1, "f1i", range(0, K1))
                        started = K1
                    state["ub"] = UB
                    kv_body(l, t)
                state["ub"] = "A"
            if stop in ("ffn1", "kv"):
                break
            state["sweepB"] = True
            if stop is not None:
                for t in range(NT):
                    state["ub"] = "A"
                    mixer(l, t)
                    if stop == "mixer":
                        continue
                    ffn(l, t, "f2i", "f2o", G2)
            else:
                state["ub"] = "A"
                P.tag = "mix.norm"
                norm(l, 0, GM)
                mixer_q(l, 0)
                mixer_c(l, 0)
                for t in range(NT):
                    state["ub"] = "A"

                    def after_merge(t=t):
                        if t + 1 < NT:
                            P.tag = "mix.norm"
                            norm(l, t + 1, GM)
                    mixer_rest(l, t, after_merge)
                    state["ub"] = "B"
                    ffn_norm(l, t, G2)
                    if t + 1 < NT:
                        state["ub"] = "A"
                        mixer_q(l, t + 1)
                        state["light"] = False
                        state["ub"] = "B"
                    ffn_in(l, t, "f2i", range(11))
                    ffn_out(l, t, "f2o")

                    def after_norm(t=t):
                        if t + 1 < NT:
                            state["ub"] = "A"
                            mixer_c(l, t + 1)
                    ple(l, t, after_norm)
                    state["light"] = False
                    state["ub"] = "A"
                    if l == NL - 1:
                        store_tile(t)
                        done = True
            if stop in ("mixer", "ffn2"):
                break
            while pending:
                issue_cast(pending.pop(0))
        if not done:
            for t in range(NT):
                store_tile(t)
        P.final_wait("sp", "out")

        semnames = list(ENGS) + sorted(P.dsem.keys())
        sems = {n: es.enter_context(nc.semaphore("s_" + n)) for n in semnames}
        block = es.enter_context(nc.Block())

        def mk(ename):
            groups = P.groups[ename]

            def body(eng):
                for kind, waits, payload in groups:
                    for k, v in waits:
                        eng.wait_ge(sems[k], v)
                    if kind == "op":
                        ins = None
                        for fn in payload:
                            ins = fn(eng)
                        ins.then_inc(sems[ename], 1)
                    elif kind == "dma":
                        out, in_, sem = payload
                        eng.dma_start(out=out, in_=in_).then_inc(sems[sem], 16)
            return body
        block.tensor(mk("pe"))
        block.scalar(mk("act"))
        block.vector(mk("dve"))
        block.gpsimd(mk("pool"))
        block.sync(mk("sp"))
    global _LAST_PROG
    _LAST_PROG = P
    return nc


_LAST_PROG = None


def _t5_buckets_np():
    import math
    try:
        return _t5_buckets_jax()
    except Exception:
        pass
    s = np.arange(128)[:, None, None]
    kb = np.arange(3)[None, :, None]
    q = np.arange(128)[None, None, :]
    rel = (kb - 1) * 128 + s - q
    n = np.abs(rel)
    ret = np.where(rel > 0, 16, 0)
    nf = np.maximum(n, 1).astype(np.float32)
    large = 8 + (np.log(nf / np.float32(8)) / np.float32(math.log(128 / 8)) * np.float32(8)).astype(np.int32)
    large = np.minimum(large, 15)
    return ret + np.where(n < 8, n, large), n <= 128


def _t5_buckets_jax():
    import jax
    import jax.numpy as jnp
    import math
    with jax.default_device(jax.devices("cpu")[0]):
        s = jnp.arange(128)[:, None, None]
        kb = jnp.arange(3)[None, :, None]
        q = jnp.arange(128)[None, None, :]
        rel = (kb - 1) * 128 + s - q
        half, max_exact = 16, 8
        n = jnp.abs(rel)
        ret = jnp.where(rel > 0, half, 0)
        nf = jnp.maximum(n, 1).astype(jnp.float32)
        large = max_exact + (jnp.log(nf / max_exact) / math.log(128 / max_exact) * (half - max_exact)).astype(jnp.int32)
        large = jnp.minimum(large, half - 1)
        bucket = ret + jnp.where(n < max_exact, n, large)
        valid = jnp.abs(rel) <= 128
        return np.asarray(bucket), np.asarray(valid)


def host_layout(rel_bias, norm_ffn1, norm_mix, q_norm, k_norm, sink, conv_w, conv_b, conv_ln_g, conv_ln_b,
                norm_ffn2, norm_pe, NL):
    f = np.float32
    vecs = np.zeros((128, NV), f)

    def chunked(v):
        C = v.shape[1] // 128
        return np.ascontiguousarray(v.reshape(NL, C, 128).transpose(2, 0, 1).reshape(128, NL * C))
    vecs[:, G1:G1 + 8 * NL] = chunked(norm_ffn1)
    vecs[:, GM:GM + 8 * NL] = chunked(norm_mix)
    vecs[:, G2:G2 + 8 * NL] = chunked(norm_ffn2)
    vecs[:, GP:GP + 8 * NL] = chunked(norm_pe)
    vecs[:, GQ:GQ + NL] = np.concatenate([q_norm, q_norm], axis=1).T
    vecs[:, GK:GK + NL] = np.concatenate([k_norm, k_norm], axis=1).T
    vecs[:, CB:CB + 4 * NL] = chunked(conv_b)
    vecs[:, LG:LG + 4 * NL] = chunked(conv_ln_g)
    vecs[:, LB:LB + 4 * NL] = chunked(conv_ln_b)
    cw = conv_w.reshape(NL, CWID, 4, 128).transpose(3, 0, 2, 1).reshape(128, NL * 4 * CWID)
    vecs[:, CWB:CWB + NL * 4 * CWID] = cw
    vecs[:, IDB:IDB + 128] = np.eye(128, dtype=f)
    vecs[0, SELB:SELB + 64] = 1.0
    vecs[1, SELB + 64:SELB + 128] = 1.0
    bucket, valid = _t5_buckets_np()
    g = rel_bias[bucket]
    g = np.where(valid[..., None], g, f(-1e9)).astype(f)
    biasT = np.ascontiguousarray(g.reshape(128, 3, 128, 2, 4).transpose(0, 1, 3, 4, 2).reshape(128, 3072))
    sr = np.broadcast_to(sink.reshape(NL, 2, 4)[:, :, :, None], (NL, 2, 4, 128)).transpose(1, 0, 2, 3)
    sinkr = np.ascontiguousarray(sr.reshape(2, NL * 512)).astype(f)
    return vecs, biasT, sinkr


_NC_CACHE = {}


def run_windows(xw, pw, params, NL, stop=None, core_ids=None):
    NTOK = xw[0].shape[0]
    key = (NTOK, NL, stop)
    if key not in _NC_CACHE:
        _NC_CACHE[key] = build(NTOK, NL, stop)
    nc = _NC_CACHE[key]
    vecs, biasT, sinkr = host_layout(params["rel_bias"], params["norm_ffn1"], params["norm_mix"], params["q_norm"],
                                     params["k_norm"], params["sink"], params["conv_w"], params["conv_b"],
                                     params["conv_ln_g"], params["conv_ln_b"], params["norm_ffn2"], params["norm_pe"], NL)
    in_maps = []
    for xi, pi in zip(xw, pw):
        m = {"xT": np.ascontiguousarray(xi.T), "pT": np.ascontiguousarray(pi.transpose(0, 2, 1)),
             "vecs": vecs, "biasT": biasT, "sinkr": sinkr}
        for k, name in WNAMES.items():
            m[name] = params[name]
        in_maps.append(m)
    if core_ids is None:
        core_ids = list(range(len(xw)))
    res = run_bass_kernel_spmd(nc, in_maps, core_ids=core_ids)
    return [np.ascontiguousarray(r["outT"].T) for r in res.results]


def kernel(x, p, rel_bias, norm_ffn1, w_ffn1_in, w_ffn1_out, norm_mix, w_in, q_norm, k_norm,
           sink, conv_w, conv_b, conv_ln_g, conv_ln_b, w_attn_out, w_conv_out, w_o,
           norm_ffn2, w_ffn2_in, w_ffn2_out, norm_pe, w_pe_gate, w_pe_proj):
    a = lambda v: np.ascontiguousarray(np.asarray(v, dtype=np.float32))
    x = a(x)
    p = a(p)
    params = dict(rel_bias=a(rel_bias), norm_ffn1=a(norm_ffn1), w_ffn1_in=a(w_ffn1_in), w_ffn1_out=a(w_ffn1_out),
                  norm_mix=a(norm_mix), w_in=a(w_in), q_norm=a(q_norm), k_norm=a(k_norm), sink=a(sink),
                  conv_w=a(conv_w), conv_b=a(conv_b), conv_ln_g=a(conv_ln_g), conv_ln_b=a(conv_ln_b),
                  w_attn_out=a(w_attn_out), w_conv_out=a(w_conv_out), w_o=a(w_o), norm_ffn2=a(norm_ffn2),
                  w_ffn2_in=a(w_ffn2_in), w_ffn2_out=a(w_ffn2_out), norm_pe=a(norm_pe), w_pe_gate=a(w_pe_gate),
                  w_pe_proj=a(w_pe_proj))
    B = x.shape[0]
    starts = [0, 1792, 3584, 5376]
    owned = [(0, 2304), (2304, 4096), (4096, 5888), (5888, 8192)]
    xw, pw, meta = [], [], []
    for b in range(B):
        for ci in range(4):
            w0 = starts[ci]
            xw.append(x[b, w0:w0 + WIN])
            pw.append(p[:, b, w0:w0 + WIN])
            meta.append((b, owned[ci][0], owned[ci][1], w0))
    outs = run_windows(xw, pw, params, NLAYER)
    out = np.empty_like(x)
    for (b, o0, o1, w0), ow in zip(meta, outs):
        out[b, o0:o1] = ow[o0 - w0:o1 - w0]
    return out
```

```python
import contextlib
import numpy as np
import concourse.bass as bass
import concourse.mybir as mybir
from concourse.bass_utils import run_bass_kernel_spmd

F32, BF16 = mybir.dt.float32, mybir.dt.bfloat16
AF = mybir.ActivationFunctionType
ALU = mybir.AluOpType

D = 1024
DFF = 2816
INDIM = 3840
PLE = 256
CWID = 31
NLAYER = 4
SEQ = 8192
WIN = 2816

G1, GM, G2, GP, GQ, GK, CB, LG, LB, CWB = 0, 32, 64, 96, 128, 132, 136, 152, 168, 184
IDB = CWB + 16 * CWID
SELB = IDB + 128
NV = SELB + 128

WNAMES = dict(f1i="w_ffn1_in", f1o="w_ffn1_out", win="w_in", ao="w_attn_out", co="w_conv_out",
              wo="w_o", f2i="w_ffn2_in", f2o="w_ffn2_out", pg="w_pe_gate", pp="w_pe_proj")
WSHAPES = dict(f1i=(D, 2 * DFF), f1o=(DFF, D), win=(D, INDIM), ao=(512, D), co=(512, D),
               wo=(D, D), f2i=(D, 2 * DFF), f2o=(DFF, D), pg=(D, D), pp=(PLE, D))
NPIECE = dict(f1i=32, f1o=16, win=16, ao=4, co=4, wo=8, f2i=32, f2o=16, pg=8, pp=2)
WORDER = ["f1i", "f1o", "win", "ao", "co", "wo", "f2i", "f2o", "pg", "pp"]
ENGS = ("pe", "act", "dve", "pool", "sp")
NS = 3


class Prog:
    def __init__(self):
        self.groups = {e: [] for e in ENGS}
        self.cnt = {e: 0 for e in ENGS}
        self.water = {e: {} for e in ENGS}
        self.buf = {}
        self.dsem = {}
        self.tag = ""
        self.tags = {e: [] for e in ENGS}

    def _deps(self, eng, reads, writes):
        deps = {}

        def add(k, v):
            if k == eng and eng in ("pe", "act", "dve"):
                return
            if deps.get(k, 0) < v:
                deps[k] = v
        for r in reads:
            st = self.buf.get(r)
            if st and st[0]:
                add(*st[0])
        for w in writes:
            st = self.buf.get(w)
            if st:
                if st[0]:
                    add(*st[0])
                for k, v in st[1].items():
                    add(k, v)
        wm = self.water[eng]
        waits = []
        for k, v in deps.items():
            if wm.get(k, 0) < v:
                wm[k] = v
                waits.append((k, v))
        return waits

    def _mark(self, tok, reads, writes):
        for r in reads:
            st = self.buf.setdefault(r, [None, {}])
            if st[1].get(tok[0], 0) < tok[1]:
                st[1][tok[0]] = tok[1]
        for w in writes:
            self.buf[w] = [tok, {}]

    def op(self, eng, fns, reads=(), writes=()):
        waits = self._deps(eng, reads, writes)
        self.cnt[eng] += 1
        tok = (eng, self.cnt[eng])
        self.groups[eng].append(("op", waits, fns))
        self.tags[eng].append((self.tag, len(fns)))
        self._mark(tok, reads, writes)

    def dma(self, eng, sem, out, in_, reads=(), writes=(), mark=True, after=()):
        waits = self._deps(eng, reads, writes)
        for k, v in after:
            if self.water[eng].get(k, 0) < v:
                self.water[eng][k] = v
                waits.append((k, v))
        self.dsem[sem] = self.dsem.get(sem, 0) + 16
        tok = (sem, self.dsem[sem])
        self.groups[eng].append(("dma", waits, (out, in_, sem)))
        if mark:
            self._mark(tok, reads, writes)
        else:
            self._mark(tok, reads, ())
        return tok

    def setw(self, keys, tok):
        for k in keys:
            self.buf[k] = [tok, {}]

    def final_wait(self, eng, sem):
        self.groups[eng].append(("wait", [(sem, self.dsem[sem])], None))


def MM(out, lhsT, rhs, start, stop):
    return lambda e: e.matmul(out, lhsT=lhsT, rhs=rhs, start=start, stop=stop)


def ACT(out, in_, func, **kw):
    return lambda e: e.activation(out=out, in_=in_, func=func, **kw)


def TT(out, a, b, op):
    return lambda e: e.tensor_tensor(out=out, in0=a, in1=b, op=op)


def STT(out, a, s, b, op0, op1):
    return lambda e: e.scalar_tensor_tensor(out=out, in0=a, scalar=s, in1=b, op0=op0, op1=op1)


def TS(out, a, s1, s2, op0, op1):
    return lambda e: e.tensor_scalar(out=out, in0=a, scalar1=s1, scalar2=s2, op0=op0, op1=op1)


def CP(out, a):
    return lambda e: e.tensor_copy(out=out, in_=a)


def RCP(out, a):
    return lambda e: e.reciprocal(out=out, in_=a)


def ACOPY(out, in_):
    return lambda e: e.copy(out=out, in_=in_)


def MSET(out, v):
    return lambda e: e.memset(out, v)


def build(NTOK, NL, stop=None):
    TW = [512] * (NTOK // 512) + ([NTOK % 512] if NTOK % 512 else [])
    if NTOK % 512 == 256 and NTOK >= 768:
        TW = [512] * (NTOK // 512 - 1) + [384, 384]
    T0 = [sum(TW[:i]) for i in range(len(TW))]
    NT = len(TW)
    NB = NTOK // 128
    assert NTOK % 128 == 0
    nc = bass.Bass("TRN2", target_bir_lowering=False)
    P = Prog()

    def din(name, shape):
        return nc.dram_tensor(name, list(shape), F32, kind="ExternalInput").ap()
    xT = din("xT", (D, NTOK))
    pT = din("pT", (NL, PLE, NTOK))
    vecs_d = din("vecs", (128, NV))
    bias_d = din("biasT", (128, 3072))
    sink_d = din("sinkr", (2, NL * 512))
    Wd = {m: din(WNAMES[m], (NL,) + WSHAPES[m]) for m in WORDER}
    Wb = {m: nc.dram_tensor("wb_" + m, [NL] + list(WSHAPES[m]), BF16, kind="Internal").ap() for m in WORDER}
    outT = nc.dram_tensor("outT", [D, NTOK], F32, kind="ExternalOutput").ap()

    es = contextlib.ExitStack()
    with es:
        def sb(name, shape, dt):
            return es.enter_context(nc.sbuf_tensor(name, list(shape), dt))
        xres = sb("xres", (128, 8, NTOK), F32)
        kT = sb("kT", (128, NTOK), BF16)
        vtok = sb("vtok", (128, NB, 128), BF16)
        vpad = sb("vpad", (128, 2, 6, 128), BF16)
        unedge = sb("unedge", (128, NT, 2, 8, 16), BF16)
        un = sb("un", (128, 8, 512), BF16)
        wkt = sb("wkt", (128, 26 * 512), BF16)
        cgx = sb("cgx", (128, 4, 544), BF16)
        biasb = sb("biasb", (128, 3072), BF16)
        vecs = sb("vecs_s", (128, NV), F32)
        gq8 = sb("gq8", (128, NL), F32)
        identb = sb("identb", (128, 128), BF16)
        sel2b = sb("sel2b", (2, 128), BF16)
        onesD = sb("onesD", (128, 128), BF16)
        ones512 = sb("ones512", (128, 128), BF16)
        bd64 = sb("bd64", (128, 128), BF16)
        onespad = sb("onespad", (128, 2, 128), BF16)
        sinks = sb("sinks", (2, NL * 512), F32)
        esrow = sb("esrow", (2, NL * 512), BF16)
        xsq = [sb(f"xsq{i}", (128, 512), BF16) for i in range(2)]
        rs = sb("rs", (128, 512), F32)
        TMP = [sb(f"tmp{i}", (128, 512), F32) for i in range(4)]
        acc = [TMP[2], TMP[3]]
        sgh = sb("sgh", (128, 32), F32)
        pTb = [sb(f"pTb{i}", (128, 2, 512), BF16) for i in range(1)]
        ringt = [sb(f"ring{i}", (128, 4096), BF16) for i in range(NS)]
        ps = [es.enter_context(nc.psum_tensor(f"ps{i}", [128, 512], F32)) for i in range(8)]

        state = dict(ps=0, ring=0, ptb=0, nload=0, w=512, ub="A", layer=0, light=False, sweepB=False)
        pending = []

        def nextps():
            i = state["ps"]
            state["ps"] = (i + 1) % 8
            return i

        def wk(i):
            return wkt[:, i * 512:i * 512 + state["w"]]

        def ring_load(parts):
            i = state["ring"] % NS
            state["ring"] += 1
            keys = []
            for (_, _, sk) in parts:
                _, m_, l_ = sk.split(":")
                idx = [n for n, pc in enumerate(pending) if pc[0] == m_ and pc[1] == int(l_)]
                if idx:
                    for _ in range(idx[-1] + 1):
                        issue_cast(pending.pop(0))
            tok = None
            for pi, (dstf, src, sk) in enumerate(parts):
                key = f"wr{i}_{pi}"
                tok = P.dma("sp", key, dstf(ringt[i]), src, reads=[sk], writes=[key])
                keys.append(key)
            state["nload"] += 1
            if pending:
                own = pending[0][1] == state["layer"]
                if own or state["light"] or state["nload"] % 3 == 0:
                    issue_cast(pending.pop(0), after=[tok])
            return ringt[i], keys

        def v8(lo, n):
            return lambda s: s[:, lo:lo + 8 * n].rearrange("p (k n) -> p k n", k=8)

        P.dma("sp", "vec", vecs[:], vecs_d[:, :], writes=["vecs"])
        P.dma("sp", "snk", sinks[:], sink_d[:, :], writes=["sinks"])
        P.dma("pool", "bia", biasb[:], bias_d[:, :], writes=["biasb"])
        xTv = xT.rearrange("(c p) n -> p c n", p=128)
        for t in range(NT):
            P.dma("sp", f"xin{t}", xres[:, :, T0[t]:T0[t] + TW[t]], xTv[:, :, T0[t]:T0[t] + TW[t]],
                  writes=[f"x:{t}:{c}" for c in range(8)])

        def cast_layer_pieces(l):
            lst = []
            for m in WORDER:
                rows = WSHAPES[m][0]
                npc = NPIECE[m]
                r = rows // npc
                for i in range(npc):
                    lst.append((m, l, i, r))
            return lst

        def issue_cast(piece, after=()):
            m, l, i, r = piece
            sem = f"c_{m}_{l}"
            if m == "win":
                for a_ in range(2):
                    P.dma("pool", sem,
                          Wb[m][l, i * r:(i + 1) * r, 0:512].rearrange("r (j a n) -> r a j n", j=4, a=2)[:, a_],
                          Wd[m][l, i * r:(i + 1) * r, a_ * 256:(a_ + 1) * 256].rearrange("r (j n) -> r j n", j=4),
                          mark=False, after=after)
                tok = P.dma("pool", sem, Wb[m][l, i * r:(i + 1) * r, 512:INDIM], Wd[m][l, i * r:(i + 1) * r, 512:INDIM],
                            mark=False, after=after)
            else:
                tok = P.dma("pool", sem, Wb[m][l, i * r:(i + 1) * r, :], Wd[m][l, i * r:(i + 1) * r, :], mark=False,
                            after=after)
            if i == NPIECE[m] - 1:
                P.setw([f"wb:{m}:{l}"], tok)

        for pc in cast_layer_pieces(0):
            if pc[0] == "f1i":
                issue_cast(pc)
            else:
                pending.append(pc)

        P.op("dve", [MSET(onesD[:], 1.0 / 1024), MSET(ones512[:], 1.0 / 512), MSET(bd64[:], 0.0),
                     MSET(onespad[:], 0.0), MSET(vpad[:], 0.0)], writes=["consts"])
        P.op("dve", [MSET(bd64[0:64, 0:64], 1.0 / 64), MSET(bd64[64:128, 64:128], 1.0 / 64),
                     MSET(onespad[:, 0, 0:64], 1.0), MSET(onespad[:, 1, 64:128], 1.0)],
             reads=["consts"], writes=["consts"])
        P.op("dve", [CP(identb[:], vecs[:, IDB:IDB + 128]), CP(sel2b[:], vecs[0:2, SELB:SELB + 128]),
                     (lambda e: e.tensor_scalar_mul(out=gq8[:], in0=vecs[:, GQ:GQ + NL], scalar1=0.125))],
             reads=["vecs", "consts"], writes=["consts"])
        P.op("act", [ACT(esrow[:], sinks[:], AF.Exp)], reads=["sinks"], writes=["esrow"])
        CONST = ["consts"]

        def tokslice(t):
            state["w"] = TW[t]
            return slice(T0[t], T0[t] + TW[t])

        def W():
            return state["w"]

        def psv(i):
            return ps[i][:, :state["w"]]

        def unv(c):
            if state["ub"] == "A":
                return un[:, c, :state["w"]]
            o = (c % 2) * 512
            return TMP[c // 2][:].bitcast(BF16)[:, o:o + state["w"]]

        def unkey(c):
            return f"un:{c}" if state["ub"] == "A" else f"tmp{c // 2}"

        def unk():
            return sorted(set(unkey(c) for c in range(8)))

        def norm(l, t, gbase):
            tok = tokslice(t)
            pi = nextps()
            for c in range(8):
                sq = xsq[c % 2]
                P.op("act", [ACT(sq[:, :W()], xres[:, c, tok], AF.Square)], reads=[f"x:{t}:{c}"], writes=[f"xsq{c % 2}"])
                P.op("pe", [MM(psv(pi), onesD[:], sq[:, :W()], c == 0, c == 7)], reads=[f"xsq{c % 2}"] + CONST,
                     writes=[f"ps{pi}"])
            P.op("act", [ACT(rs[:, :W()], psv(pi), AF.Ln, bias=1e-6, scale=1.0)], reads=[f"ps{pi}"], writes=["rs"])
            P.op("act", [ACT(rs[:, :W()], rs[:, :W()], AF.Exp, scale=-0.5)], reads=["rs"], writes=["rs"])
            for c in range(8):
                g = vecs[:, gbase + l * 8 + c: gbase + l * 8 + c + 1]
                P.op("dve", [STT(unv(c), xres[:, c, tok], g, rs[:, :W()], ALU.mult, ALU.mult)],
                     reads=[f"x:{t}:{c}", "rs", "vecs"], writes=[unkey(c)])

        def ffn(l, t, m_in, m_out, gbase):
            ffn_norm(l, t, gbase)
            ffn_in(l, t, m_in, range(11))
            ffn_out(l, t, m_out)

        def ffn_norm(l, t, gbase):
            P.tag = "ffn.norm"
            norm(l, t, gbase)

        def ffn_in(l, t, m_in, pairs):
            tok = tokslice(t)
            P.tag = "ffn.in"
            win = Wb[m_in][l].rearrange("(kc p) n -> p kc n", p=128)
            kin = f"wb:{m_in}:{l}"
            for i in pairs:
                slot, keys = ring_load([(v8(0, 256), win[:, :, 256 * i:256 * i + 256], kin),
                                        (v8(2048, 256), win[:, :, DFF + 256 * i:DFF + 256 * i + 256], kin)])
                for a in range(2):
                    fc = 2 * i + a
                    pg, pu = nextps(), nextps()
                    ins = []
                    for kc in range(8):
                        o = kc * 256 + a * 128
                        ins.append(MM(psv(pg), slot[:, o:o + 128], unv(kc), kc == 0, kc == 7))
                    for kc in range(8):
                        o = 2048 + kc * 256 + a * 128
                        ins.append(MM(psv(pu), slot[:, o:o + 128], unv(kc), kc == 0, kc == 7))
                    P.op("pe", ins, reads=keys + unk(), writes=[f"ps{pg}", f"ps{pu}"])
                    P.op("act", [ACT(wk(fc), psv(pg), AF.Silu)], reads=[f"ps{pg}"], writes=[f"wk:{fc}"])
                    P.op("dve", [TT(wk(fc), wk(fc), psv(pu), ALU.mult)], reads=[f"wk:{fc}", f"ps{pu}"],
                         writes=[f"wk:{fc}"])

        def ffn_out(l, t, m_out):
            tok = tokslice(t)
            P.tag = "ffn.out"
            wout = Wb[m_out][l].rearrange("(fc p) n -> p fc n", p=128)
            kout = f"wb:{m_out}:{l}"
            for i in range(4):
                po = [nextps(), nextps()]
                for hh in range(2):
                    slot, keys = ring_load([(lambda s: s[:, 0:2816].rearrange("p (f n) -> p f n", f=11),
                                             wout[:, 11 * hh:11 * hh + 11, 256 * i:256 * i + 256], kout)])
                    ins = []
                    for a in range(2):
                        for f in range(11):
                            o = f * 256 + a * 128
                            ins.append(MM(psv(po[a]), slot[:, o:o + 128], wk(11 * hh + f),
                                          hh == 0 and f == 0, hh == 1 and f == 10))
                    P.op("pe", ins, reads=keys + [f"wk:{11 * hh + f}" for f in range(11)],
                         writes=[f"ps{po[0]}", f"ps{po[1]}"])
                for a in range(2):
                    oc = 2 * i + a
                    P.op("dve", [STT(xres[:, oc, tok], psv(po[a]), 0.5, xres[:, oc, tok], ALU.mult, ALU.add)],
                         reads=[f"ps{po[a]}", f"x:{t}:{oc}"], writes=[f"x:{t}:{oc}"])

        def qknorm(pq, gcol_ap, out_ap, outkeys, blk=False):
            vw = (lambda a: a.rearrange("p (b q) -> p b q", b=W() // 128)) if blk else (lambda a: a)
            P.op("act", [ACT(xsq[0][:, :W()], psv(pq), AF.Square)], reads=[f"ps{pq}"], writes=["xsq0"])
            pst = nextps()
            P.op("pe", [MM(psv(pst), bd64[:], xsq[0][:, :W()], True, True)], reads=["xsq0"] + CONST, writes=[f"ps{pst}"])
            P.op("act", [ACT(rs[:, :W()], psv(pst), AF.Ln, bias=1e-6, scale=1.0)], reads=[f"ps{pst}"], writes=["rs"])
            P.op("act", [ACT(rs[:, :W()], rs[:, :W()], AF.Exp, scale=-0.5)], reads=["rs"], writes=["rs"])
            P.op("dve", [STT(out_ap, vw(psv(pq)), gcol_ap, vw(rs[:, :W()]), ALU.mult, ALU.mult)],
                 reads=[f"ps{pq}", "rs", "vecs"] + CONST, writes=outkeys)

        def kv_norm(l, t):
            P.tag = "kv"
            norm(l, t, GM)

        def kv_phase(l, t):
            kv_norm(l, t)
            kv_body(l, t)

        def kv_body(l, t):
            tok = tokslice(t)
            P.tag = "kv"
            P.op("pool", [CP(unedge[:, t, 0, c, :], unv(c)[:, 0:16]) for c in range(8)] +
                 [CP(unedge[:, t, 1, c, :], unv(c)[:, W() - 16:W()]) for c in range(8)],
                 reads=unk(), writes=[f"ue:{t}"])
            win = Wb["win"][l].rearrange("(kc p) n -> p kc n", p=128)
            slot, keys = ring_load([(v8(0, 256), win[:, :, 512:768], f"wb:win:{l}")])
            pk = nextps()
            P.op("pe", [MM(psv(pk), slot[:, kc * 256:kc * 256 + 128], unv(kc), kc == 0, kc == 7)
                        for kc in range(8)], reads=keys + unk(), writes=[f"ps{pk}"])
            pv = nextps()
            ins = []
            nb = W() // 128
            B0 = T0[t] // 128
            for b in range(nb):
                for kc in range(8):
                    ins.append(MM(ps[pv][:, b * 128:(b + 1) * 128], unv(kc)[:, b * 128:(b + 1) * 128],
                                  slot[:, kc * 256 + 128:kc * 256 + 256], kc == 0, kc == 7))
            P.op("pe", ins, reads=keys + unk(), writes=[f"ps{pv}"])
            qknorm(pk, vecs[:, GK + l:GK + l + 1], kT[:, tok], [f"k:{B0 + b}" for b in range(nb)])
            P.op("act", [ACOPY(vtok[:, B0:B0 + nb, :], psv(pv).rearrange("p (b n) -> p b n", b=nb))],
                 reads=[f"ps{pv}"], writes=[f"v:{B0 + b}" for b in range(nb)])

        ESL = [[0, 1, 2, 3, 20, 21], [12, 13, 14, 15, 16, 17]]
        QO = 22

        def qo_view():
            return wkt[:, QO * 512:(QO + 4) * 512].rearrange("p (j t) -> p j t", j=4)

        def mixer(l, t, prenormed=False):
            P.tag = "mix.norm"
            state["light"] = True
            if not prenormed:
                norm(l, t, GM)
            mixer_q(l, t)
            mixer_c(l, t)
            mixer_rest(l, t, lambda: None)

        def mixer_q(l, t):
            tok = tokslice(t)
            state["light"] = True
            P.tag = "mix.q"
            win = Wb["win"][l].rearrange("(kc p) n -> p kc n", p=128)
            kwin = f"wb:win:{l}"
            slot, keys = ring_load([(v8(0, 256), win[:, :, 0:256], kwin), (v8(2048, 256), win[:, :, 256:512], kwin)])
            qo4 = wkt[:, QO * 512:(QO + 4) * 512].rearrange("p (b j q) -> p b j q", b=4, j=4)
            for j in range(4):
                pq = nextps()
                o = (j // 2) * 2048 + (j % 2) * 128
                P.op("pe", [MM(psv(pq), slot[:, o + kc * 256:o + kc * 256 + 128], unv(kc), kc == 0, kc == 7)
                            for kc in range(8)], reads=keys + unk(), writes=[f"ps{pq}"])
                qknorm(pq, gq8[:, l:l + 1], qo4[:, :W() // 128, j, :], [f"qo:{j}:{b}" for b in range(W() // 128)], blk=True)

        def mixer_c(l, t):
            tok = tokslice(t)
            state["light"] = True
            win = Wb["win"][l].rearrange("(kc p) n -> p kc n", p=128)
            kwin = f"wb:win:{l}"
            sgs = wkt[:, 4 * 512:6 * 512].bitcast(F32)[:, :W()]
            sgk = ["wk:4", "wk:5"]
            P.tag = "mix.cproj"
            if t == 0:
                P.op("pool", [MSET(cgx[:, :, 0:16], 0.0)], writes=[f"cg:{ch}" for ch in range(4)])
            if t == NT - 1:
                P.op("pool", [MSET(cgx[:, :, 16 + W():32 + W()], 0.0)], writes=[f"cg:{ch}" for ch in range(4)])
            sv_, kv_ = ring_load([(v8(0, 256), win[:, :, 768:1024], kwin), (v8(2048, 256), win[:, :, 1024:1280], kwin)])
            sg_, kg_ = ring_load([(v8(0, 256), win[:, :, 1280:1536], kwin), (v8(2048, 256), win[:, :, 1536:1792], kwin)])
            for ch in range(4):
                o = (ch // 2) * 2048 + (ch % 2) * 128
                pval, pgate, ph = nextps(), nextps(), nextps()
                ins = []
                for kc in range(8):
                    ins.append(MM(psv(pval), sv_[:, o + kc * 256:o + kc * 256 + 128], unv(kc), kc == 0, kc == 7))
                for kc in range(8):
                    ins.append(MM(psv(pgate), sg_[:, o + kc * 256:o + kc * 256 + 128], unv(kc), kc == 0, kc == 7))
                halos = []
                if t > 0:
                    halos.append((0, unedge[:, t - 1, 1, :, :], f"ue:{t - 1}"))
                if t < NT - 1:
                    halos.append((1, unedge[:, t + 1, 0, :, :], f"ue:{t + 1}"))
                for side, ue, uk in halos:
                    for kc in range(8):
                        ins.append(MM(ps[ph][:, side * 16:side * 16 + 16], sv_[:, o + kc * 256:o + kc * 256 + 128],
                                      ue[:, kc, :], kc == 0, kc == 7))
                    for kc in range(8):
                        ins.append(MM(ps[ph][:, 32 + side * 16:32 + side * 16 + 16],
                                      sg_[:, o + kc * 256:o + kc * 256 + 128], ue[:, kc, :], kc == 0, kc == 7))
                P.op("pe", ins, reads=kv_ + kg_ + unk() + [h[2] for h in halos],
                     writes=[f"ps{pval}", f"ps{pgate}", f"ps{ph}"])
                fa = [ACT(sgs, psv(pgate), AF.Sigmoid)]
                fd = [TT(cgx[:, ch, 16:16 + W()], psv(pval), sgs, ALU.mult)]
                for side, ue, uk in halos:
                    fa.append(ACT(sgh[:, side * 16:16 + side * 16], ps[ph][:, 32 + side * 16:48 + side * 16], AF.Sigmoid))
                    dst = cgx[:, ch, 0:16] if side == 0 else cgx[:, ch, 16 + W():32 + W()]
                    fd.append(TT(dst, ps[ph][:, side * 16:side * 16 + 16], sgh[:, side * 16:16 + side * 16], ALU.mult))
                P.op("act", fa, reads=[f"ps{pgate}", f"ps{ph}"], writes=sgk + ["sgh"])
                P.op("dve", fd, reads=sgk + ["sgh", f"ps{pval}", f"ps{ph}"], writes=[f"cg:{ch}"])

        def mixer_rest(l, t, after_merge):
            tok = tokslice(t)
            state["light"] = True
            win = Wb["win"][l].rearrange("(kc p) n -> p kc n", p=128)
            kwin = f"wb:win:{l}"
            P.tag = "mix.attn"
            oT = wkt[:, 4 * 512:8 * 512].rearrange("p (j t) -> p j t", j=4)
            wtile = W()
            state["w"] = 512
            nblk = wtile // 128

            def att_S(b):
                B = T0[t] // 128 + b
                st_ = B % 2
                kbs = [kb for kb in range(3) if 0 <= B + kb - 1 < NB]
                cps = []
                for kb in kbs:
                    cps.append(CP(vpad[:, st_, kb * 2 + 0, 0:64], vtok[:, B + kb - 1, 0:64]))
                    cps.append(CP(vpad[:, st_, kb * 2 + 1, 64:128], vtok[:, B + kb - 1, 64:128]))
                P.op("pool", cps, reads=[f"v:{B + kb - 1}" for kb in kbs] + CONST, writes=[f"vp:{st_}"])
                slist = []
                for kh in range(2):
                    for kb in kbs:
                        si = nextps()
                        KB = B + kb - 1
                        P.op("pe", [MM(psv(si), kT[kh * 64:(kh + 1) * 64, KB * 128:(KB + 1) * 128],
                                       wkt[kh * 64:(kh + 1) * 64, (QO + b) * 512:(QO + b + 1) * 512], True, False),
                                    MM(psv(si), identb[:], biasb[:, (kb * 2 + kh) * 512:(kb * 2 + kh + 1) * 512], False, True)],
                             reads=[f"k:{KB}", "biasb"] + [f"qo:{j}:{b}" for j in range(4)] + CONST, writes=[f"ps{si}"])
                        slist.append((kh, kb, ESL[b % 2][kh * 3 + kb], si))
                return slist

            def att_E(slist):
                for kh, kb, ei, si in slist:
                    P.op("act", [ACT(wk(ei), psv(si), AF.Exp)], reads=[f"ps{si}"], writes=[f"wk:{ei}"])

            def att_PV(b, elist):
                B = T0[t] // 128 + b
                st_ = B % 2
                po, pd = nextps(), nextps()
                ins = []
                for n, (kh, kb, ei, si) in enumerate(elist):
                    ins.append(MM(psv(po), vpad[:, st_, kb * 2 + kh, :], wk(ei), n == 0, n == len(elist) - 1))
                for n, (kh, kb, ei, si) in enumerate(elist):
                    ins.append(MM(psv(pd), onespad[:, kh, :], wk(ei), n == 0, False))
                ins.append(MM(psv(pd), sel2b[0:2, :], esrow[0:2, l * 512:(l + 1) * 512], False, True))
                P.op("pe", ins, reads=[f"wk:{e[2]}" for e in elist] + [f"vp:{st_}", "esrow"] + CONST,
                     writes=[f"ps{po}", f"ps{pd}"])
                P.op("act", [ACT(TMP[2][:, :W()], psv(pd), AF.Ln)], reads=[f"ps{pd}"], writes=["tmp2"])
                P.op("act", [ACT(TMP[2][:, :W()], TMP[2][:, :W()], AF.Exp, scale=-1.0)], reads=["tmp2"], writes=["tmp2"])
                P.op("dve", [TT(oT[:, :, b * 128:(b + 1) * 128], psv(po).rearrange("p (j q) -> p j q", j=4),
                                TMP[2][:, :W()].rearrange("p (j q) -> p j q", j=4), ALU.mult)],
                     reads=[f"ps{po}", "tmp2"] + [f"wk:{4 + j}" for j in range(4)], writes=[f"wk:{4 + j}" for j in range(4)])

            cur = att_S(0)
            att_E(cur)
            state["w"] = wtile
            P.tag = "mix.conv"
            dgs = [wkt[:, 12 * 512:16 * 512].rearrange("p (k n) -> p k n", k=16),
                   wkt[:, 16 * 512:20 * 512].rearrange("p (k n) -> p k n", k=16)]
            dkeys = [[f"wk:{c}" for c in range(12, 16)], [f"wk:{c}" for c in range(16, 20)]]
            for ch in range(4):
                pc = nextps()
                for hf in range(2):
                    taps = list(range(16)) if hf == 0 else list(range(16, CWID))
                    bld = []
                    for k, tap in enumerate(taps):
                        wcol = CWB + (l * 4 + ch) * CWID + tap
                        bld.append((lambda o, w_: (lambda e: e.tensor_scalar_mul(out=o, in0=identb[:], scalar1=w_)))(
                            dgs[hf][:, k, :], vecs[:, wcol:wcol + 1]))
                    P.op("dve", bld, reads=["vecs"] + CONST + dkeys[hf], writes=dkeys[hf])
                    P.op("pe", [MM(psv(pc), dgs[hf][:, k, :], cgx[:, ch, tap + 1:tap + 1 + W()], tap == 0, tap == CWID - 1)
                                for k, tap in enumerate(taps)], reads=dkeys[hf] + [f"cg:{ch}"], writes=[f"ps{pc}"])
                bcol = CB + l * 4 + ch
                P.op("act", [ACT(wk(8 + ch), psv(pc), AF.Identity, bias=vecs[:, bcol:bcol + 1], scale=1.0)],
                     reads=[f"ps{pc}", "vecs"], writes=[f"wk:{8 + ch}"])
            P.tag = "mix.attn"
            state["w"] = 512
            for b in range(nblk):
                nxt = att_S(b + 1) if b + 1 < nblk else None
                att_PV(b, cur)
                if nxt is not None:
                    att_E(nxt)
                cur = nxt
            P.tag = "mix.merge"
            state["w"] = wtile
            P.tag = "mix.ln"
            pm, pe2 = nextps(), nextps()
            for ch in range(4):
                P.op("pe", [MM(psv(pm), ones512[:], wk(8 + ch), ch == 0, ch == 3)], reads=[f"wk:{8 + ch}"] + CONST,
                     writes=[f"ps{pm}"])
                P.op("act", [ACT(xsq[ch % 2][:, :W()], wk(8 + ch), AF.Square)], reads=[f"wk:{8 + ch}"], writes=[f"xsq{ch % 2}"])
                P.op("pe", [MM(psv(pe2), ones512[:], xsq[ch % 2][:, :W()], ch == 0, ch == 3)], reads=[f"xsq{ch % 2}"] + CONST,
                     writes=[f"ps{pe2}"])
            mu, var = TMP[0], TMP[1]
            P.op("act", [ACOPY(mu[:, :W()], psv(pm))], reads=[f"ps{pm}"], writes=["tmp0"])
            P.op("dve", [TT(var[:, :W()], mu[:, :W()], mu[:, :W()], ALU.mult)], reads=["tmp0"], writes=["tmp1"])
            P.op("dve", [TT(var[:, :W()], psv(pe2), var[:, :W()], ALU.subtract)], reads=["tmp1", f"ps{pe2}"], writes=["tmp1"])
            P.op("act", [ACT(var[:, :W()], var[:, :W()], AF.Ln, bias=1e-5, scale=1.0)], reads=["tmp1"], writes=["tmp1"])
            P.op("act", [ACT(var[:, :W()], var[:, :W()], AF.Exp, scale=-0.5)], reads=["tmp1"], writes=["tmp1"])
            for ch in range(4):
                tc_ = TMP[2 + ch % 2]
                tk = f"tmp{2 + ch % 2}"
                P.op("dve", [TT(tc_[:, :W()], wk(8 + ch), mu[:, :W()], ALU.subtract)], reads=[f"wk:{8 + ch}", "tmp0"], writes=[tk])
                P.op("dve", [TT(tc_[:, :W()], tc_[:, :W()], var[:, :W()], ALU.mult)], reads=[tk, "tmp1"], writes=[tk])
                gcol, bcol = LG + l * 4 + ch, LB + l * 4 + ch
                P.op("act", [ACT(wk(8 + ch), tc_[:, :W()], AF.Silu, scale=vecs[:, gcol:gcol + 1], bias=vecs[:, bcol:bcol + 1])],
                     reads=[tk, "vecs"], writes=[f"wk:{8 + ch}"])
            state["w"] = wtile
            wao = Wb["ao"][l]
            wco = Wb["co"][l].rearrange("(c p) n -> p c n", p=128)
            for i in range(4):
                cs_ = slice(256 * i, 256 * i + 256)
                s2, k2 = ring_load([(v8(0, 256), win[:, :, 1792 + 256 * i:1792 + 256 * i + 256], kwin),
                                    (v8(2048, 256), win[:, :, 2816 + 256 * i:2816 + 256 * i + 256], kwin)])
                for a in range(2):
                    pga, pgc = nextps(), nextps()
                    ins = []
                    for kc in range(8):
                        o = kc * 256 + a * 128
                        ins.append(MM(psv(pga), s2[:, o:o + 128], unv(kc), kc == 0, kc == 7))
                    for kc in range(8):
                        o = 2048 + kc * 256 + a * 128
                        ins.append(MM(psv(pgc), s2[:, o:o + 128], unv(kc), kc == 0, kc == 7))
                    P.op("pe", ins, reads=k2 + unk(), writes=[f"ps{pga}", f"ps{pgc}"])
                    P.op("act", [ACT(wk(2 * a), psv(pga), AF.Sigmoid)], reads=[f"ps{pga}"], writes=[f"wk:{2 * a}"])
                    P.op("act", [ACT(wk(2 * a + 1), psv(pgc), AF.Sigmoid)], reads=[f"ps{pgc}"], writes=[f"wk:{2 * a + 1}"])
                s1, k1 = ring_load([
                    (lambda s: s[0:64, 0:1024].rearrange("p (j n) -> p j n", j=4),
                     wao[0:256, :].rearrange("(j p) n -> p j n", p=64)[:, :, cs_], f"wb:ao:{l}"),
                    (lambda s: s[64:128, 0:1024].rearrange("p (j n) -> p j n", j=4),
                     wao[256:512, :].rearrange("(j p) n -> p j n", p=64)[:, :, cs_], f"wb:ao:{l}"),
                    (lambda s: s[:, 1024:2048].rearrange("p (j n) -> p j n", j=4), wco[:, :, cs_], f"wb:co:{l}")])
                for a in range(2):
                    oc = 2 * i + a
                    pya, pyc = nextps(), nextps()
                    ins = []
                    for j in range(4):
                        o = j * 256 + a * 128
                        ins.append(MM(psv(pya), s1[:, o:o + 128], wk(4 + j), j == 0, j == 3))
                    for c in range(4):
                        o = 1024 + c * 256 + a * 128
                        ins.append(MM(psv(pyc), s1[:, o:o + 128], wk(8 + c), c == 0, c == 3))
                    P.op("pe", ins, reads=k1 + [f"wk:{4 + j}" for j in range(4)] + [f"wk:{8 + c}" for c in range(4)],
                         writes=[f"ps{pya}", f"ps{pyc}"])
                    ta, tb = TMP[0], TMP[1]
                    P.op("dve", [TT(ta[:, :W()], psv(pya), wk(2 * a), ALU.mult)], reads=[f"ps{pya}", f"wk:{2 * a}"], writes=["tmp0"])
                    P.op("dve", [TT(tb[:, :W()], psv(pyc), wk(2 * a + 1), ALU.mult)], reads=[f"ps{pyc}", f"wk:{2 * a + 1}"], writes=["tmp1"])
                    P.op("dve", [TT(wk(12 + oc), ta[:, :W()], tb[:, :W()], ALU.add)], reads=["tmp0", "tmp1"], writes=[f"wk:{12 + oc}"])
            after_merge()
            tok = tokslice(t)
            P.tag = "mix.wo"
            state["light"] = False
            wo = Wb["wo"][l].rearrange("(kc p) n -> p kc n", p=128)
            for i2 in range(2):
                slot, keys = ring_load([(v8(0, 256), wo[:, :, 512 * i2:512 * i2 + 256], f"wb:wo:{l}"),
                                        (v8(2048, 256), wo[:, :, 512 * i2 + 256:512 * i2 + 512], f"wb:wo:{l}")])
                for q4 in range(4):
                    oc = 4 * i2 + q4
                    o0 = (q4 // 2) * 2048 + (q4 % 2) * 128
                    po = nextps()
                    P.op("pe", [MM(psv(po), slot[:, o0 + kc * 256:o0 + kc * 256 + 128], wk(12 + kc), kc == 0, kc == 7)
                                for kc in range(8)], reads=keys + [f"wk:{12 + kc}" for kc in range(8)], writes=[f"ps{po}"])
                    P.op("dve", [TT(xres[:, oc, tok], psv(po), xres[:, oc, tok], ALU.add)],
                         reads=[f"ps{po}", f"x:{t}:{oc}"], writes=[f"x:{t}:{oc}"])

        def ple(l, t, after_norm=None):
            tok = tokslice(t)
            P.tag = "ple"
            state["light"] = True
            norm(l, t, GP)
            if after_norm is not None:
                ub_ = state["ub"]
                after_norm()
                state["ub"] = ub_
                tok = tokslice(t)
                P.tag = "ple"
            pi = 0
            state["ptb"] += 1
            P.dma("pool", f"pt{pi}", pTb[pi][:, :, :W()], pT[l].rearrange("(kc p) t -> p kc t", p=128)[:, :, tok],
                  writes=[f"ptb{pi}"])
            wpp = Wb["pp"][l].rearrange("(kc p) n -> p kc n", p=128)
            wpg = Wb["pg"][l].rearrange("(kc p) n -> p kc n", p=128)
            for i in range(4):
                cs_ = slice(256 * i, 256 * i + 256)
                slot, keys = ring_load([(v8(0, 256), wpg[:, :, cs_], f"wb:pg:{l}"),
                                        (lambda s: s[:, 2048:2560].rearrange("p (k n) -> p k n", k=2), wpp[:, :, cs_], f"wb:pp:{l}")])
                for a in range(2):
                    oc = 2 * i + a
                    pgt, ppe = nextps(), nextps()
                    ins = [MM(psv(pgt), slot[:, kc * 256 + a * 128:kc * 256 + a * 128 + 128], unv(kc), kc == 0, kc == 7)
                           for kc in range(8)]
                    ins += [MM(psv(ppe), slot[:, 2048 + kc * 256 + a * 128:2048 + kc * 256 + a * 128 + 128], pTb[pi][:, kc, :W()],
                               kc == 0, kc == 1) for kc in range(2)]
                    P.op("pe", ins, reads=keys + unk() + [f"ptb{pi}"], writes=[f"ps{pgt}", f"ps{ppe}"])
                    tq = wkt[:, a * 1024:(a + 1) * 1024].bitcast(F32)[:, :W()]
                    tqk = [f"wk:{2 * a}", f"wk:{2 * a + 1}"]
                    P.op("act", [ACT(tq, psv(pgt), AF.Sigmoid)], reads=[f"ps{pgt}"], writes=tqk)
                    P.op("dve", [TT(tq, psv(ppe), tq, ALU.mult)], reads=[f"ps{ppe}"] + tqk, writes=tqk)
                    P.op("dve", [TT(xres[:, oc, tok], tq, xres[:, oc, tok], ALU.add)],
                         reads=tqk + [f"x:{t}:{oc}"], writes=[f"x:{t}:{oc}"])

        def store_tile(t):
            tok = tokslice(t)
            for c in range(8):
                P.dma("sp", "out", outT[c * 128:(c + 1) * 128, tok], xres[:, c, tok], reads=[f"x:{t}:{c}"])

        done = False
        for l in range(NL):
            state["layer"] = l
            state["sweepB"] = False
            pending.extend(cast_layer_pieces(l + 1) if l + 1 < NL else [])
            if stop == "ffn1":
                for t in range(NT):
                    ffn(l, t, "f1i", "f1o", G1)
            else:
                K1 = 3
                UB = "B"
                state["ub"] = "A"
                ffn_norm(l, 0, G1)
                started = 0
                for t in range(NT):
                    state["ub"] = "A"
                    ffn_in(l, t, "f1i", range(started, 11))
                    if t + 1 < NT:
                        ffn_norm(l, t + 1, G1)
                    ffn_out(l, t, "f1o")
                    state["ub"] = UB
                    kv_norm(l, t)
                    started = 0
                    if t + 1 < NT:
                        state["ub"] = "A"
                        ffn_in(l, t + 1, "f1i", range(0, K1))
                        started = K1
                    state["ub"] = UB
                    kv_body(l, t)
                state["ub"] = "A"
            if stop in ("ffn1", "kv"):
                break
            state["sweepB"] = True
            if stop is not None:
                for t in range(NT):
                    state["ub"] = "A"
                    mixer(l, t)
                    if stop == "mixer":
                        continue
                    ffn(l, t, "f2i", "f2o", G2)
            else:
                state["ub"] = "A"
                P.tag = "mix.norm"
                norm(l, 0, GM)
                mixer_q(l, 0)
                mixer_c(l, 0)
                for t in range(NT):
                    state["ub"] = "A"

                    def after_merge(t=t):
                        if t + 1 < NT:
                            P.tag = "mix.norm"
                            norm(l, t + 1, GM)
                    mixer_rest(l, t, after_merge)
                    state["ub"] = "B"
                    ffn_norm(l, t, G2)
                    if t + 1 < NT:
                        state["ub"] = "A"
                        mixer_q(l, t + 1)
                        state["light"] = False
                        state["ub"] = "B"
                    ffn_in(l, t, "f2i", range(11))
                    ffn_out(l, t, "f2o")

                    def after_norm(t=t):
                        if t + 1 < NT:
                            state["ub"] = "A"
                            mixer_c(l, t + 1)
                    ple(l, t, after_norm)
                    state["light"] = False
                    state["ub"] = "A"
                    if l == NL - 1:
                        store_tile(t)
                        done = True
            if stop in ("mixer", "ffn2"):
                break
            while pending:
                issue_cast(pending.pop(0))
        if not done:
            for t in range(NT):
                store_tile(t)
        P.final_wait("sp", "out")

        semnames = list(ENGS) + sorted(P.dsem.keys())
        sems = {n: es.enter_context(nc.semaphore("s_" + n)) for n in semnames}
        block = es.enter_context(nc.Block())

        def mk(ename):
            groups = P.groups[ename]

            def body(eng):
                for kind, waits, payload in groups:
                    for k, v in waits:
                        eng.wait_ge(sems[k], v)
                    if kind == "op":
                        ins = None
                        for fn in payload:
                            ins = fn(eng)
                        ins.then_inc(sems[ename], 1)
                    elif kind == "dma":
                        out, in_, sem = payload
                        eng.dma_start(out=out, in_=in_).then_inc(sems[sem], 16)
            return body
        block.tensor(mk("pe"))
        block.scalar(mk("act"))
        block.vector(mk("dve"))
        block.gpsimd(mk("pool"))
        block.sync(mk("sp"))
    global _LAST_PROG
    _LAST_PROG = P
    return nc


_LAST_PROG = None


def _t5_buckets_np():
    import math
    try:
        return _t5_buckets_jax()
    except Exception:
        pass
    s = np.arange(128)[:, None, None]
    kb = np.arange(3)[None, :, None]
    q = np.arange(128)[None, None, :]
    rel = (kb - 1) * 128 + s - q
    n = np.abs(rel)
    ret = np.where(rel > 0, 16, 0)
    nf = np.maximum(n, 1).astype(np.float32)
    large = 8 + (np.log(nf / np.float32(8)) / np.float32(math.log(128 / 8)) * np.float32(8)).astype(np.int32)
    large = np.minimum(large, 15)
    return ret + np.where(n < 8, n, large), n <= 128


def _t5_buckets_jax():
    import jax
    import jax.numpy as jnp
    import math
    with jax.default_device(jax.devices("cpu")[0]):
        s = jnp.arange(128)[:, None, None]
        kb = jnp.arange(3)[None, :, None]
        q = jnp.arange(128)[None, None, :]
        rel = (kb - 1) * 128 + s - q
        half, max_exact = 16, 8
        n = jnp.abs(rel)
        ret = jnp.where(rel > 0, half, 0)
        nf = jnp.maximum(n, 1).astype(jnp.float32)
        large = max_exact + (jnp.log(nf / max_exact) / math.log(128 / max_exact) * (half - max_exact)).astype(jnp.int32)
        large = jnp.minimum(large, half - 1)
        bucket = ret + jnp.where(n < max_exact, n, large)
        valid = jnp.abs(rel) <= 128
        return np.asarray(bucket), np.asarray(valid)


def host_layout(rel_bias, norm_ffn1, norm_mix, q_norm, k_norm, sink, conv_w, conv_b, conv_ln_g, conv_ln_b,
                norm_ffn2, norm_pe, NL):
    f = np.float32
    vecs = np.zeros((128, NV), f)

    def chunked(v):
        C = v.shape[1] // 128
        return np.ascontiguousarray(v.reshape(NL, C, 128).transpose(2, 0, 1).reshape(128, NL * C))
    vecs[:, G1:G1 + 8 * NL] = chunked(norm_ffn1)
    vecs[:, GM:GM + 8 * NL] = chunked(norm_mix)
    vecs[:, G2:G2 + 8 * NL] = chunked(norm_ffn2)
    vecs[:, GP:GP + 8 * NL] = chunked(norm_pe)
    vecs[:, GQ:GQ + NL] = np.concatenate([q_norm, q_norm], axis=1).T
    vecs[:, GK:GK + NL] = np.concatenate([k_norm, k_norm], axis=1).T
    vecs[:, CB:CB + 4 * NL] = chunked(conv_b)
    vecs[:, LG:LG + 4 * NL] = chunked(conv_ln_g)
    vecs[:, LB:LB + 4 * NL] = chunked(conv_ln_b)
    cw = conv_w.reshape(NL, CWID, 4, 128).transpose(3, 0, 2, 1).reshape(128, NL * 4 * CWID)
    vecs[:, CWB:CWB + NL * 4 * CWID] = cw
    vecs[:, IDB:IDB + 128] = np.eye(128, dtype=f)
    vecs[0, SELB:SELB + 64] = 1.0
    vecs[1, SELB + 64:SELB + 128] = 1.0
    bucket, valid = _t5_buckets_np()
    g = rel_bias[bucket]
    g = np.where(valid[..., None], g, f(-1e9)).astype(f)
    biasT = np.ascontiguousarray(g.reshape(128, 3, 128, 2, 4).transpose(0, 1, 3, 4, 2).reshape(128, 3072))
    sr = np.broadcast_to(sink.reshape(NL, 2, 4)[:, :, :, None], (NL, 2, 4, 128)).transpose(1, 0, 2, 3)
    sinkr = np.ascontiguousarray(sr.reshape(2, NL * 512)).astype(f)
    return vecs, biasT, sinkr


_NC_CACHE = {}


def run_windows(xw, pw, params, NL, stop=None, core_ids=None):
    NTOK = xw[0].shape[0]
    key = (NTOK, NL, stop)
    if key not in _NC_CACHE:
        _NC_CACHE[key] = build(NTOK, NL, stop)
    nc = _NC_CACHE[key]
    vecs, biasT, sinkr = host_layout(params["rel_bias"], params["norm_ffn1"], params["norm_mix"], params["q_norm"],
                                     params["k_norm"], params["sink"], params["conv_w"], params["conv_b"],
                                     params["conv_ln_g"], params["conv_ln_b"], params["norm_ffn2"], params["norm_pe"], NL)
    in_maps = []
    for xi, pi in zip(xw, pw):
        m = {"xT": np.ascontiguousarray(xi.T), "pT": np.ascontiguousarray(pi.transpose(0, 2, 1)),
             "vecs": vecs, "biasT": biasT, "sinkr": sinkr}
        for k, name in WNAMES.items():
            m[name] = params[name]
        in_maps.append(m)
    if core_ids is None:
        core_ids = list(range(len(xw)))
    res = run_bass_kernel_spmd(nc, in_maps, core_ids=core_ids)
    return [np.ascontiguousarray(r["outT"].T) for r in res.results]


def kernel(x, p, rel_bias, norm_ffn1, w_ffn1_in, w_ffn1_out, norm_mix, w_in, q_norm, k_norm,
           sink, conv_w, conv_b, conv_ln_g, conv_ln_b, w_attn_out, w_conv_out, w_o,
           norm_ffn2, w_ffn2_in, w_ffn2_out, norm_pe, w_pe_gate, w_pe_proj):
    a = lambda v: np.ascontiguousarray(np.asarray(v, dtype=np.float32))
    x = a(x)
    p = a(p)
    params = dict(rel_bias=a(rel_bias), norm_ffn1=a(norm_ffn1), w_ffn1_in=a(w_ffn1_in), w_ffn1_out=a(w_ffn1_out),
                  norm_mix=a(norm_mix), w_in=a(w_in), q_norm=a(q_norm), k_norm=a(k_norm), sink=a(sink),
                  conv_w=a(conv_w), conv_b=a(conv_b), conv_ln_g=a(conv_ln_g), conv_ln_b=a(conv_ln_b),
                  w_attn_out=a(w_attn_out), w_conv_out=a(w_conv_out), w_o=a(w_o), norm_ffn2=a(norm_ffn2),
                  w_ffn2_in=a(w_ffn2_in), w_ffn2_out=a(w_ffn2_out), norm_pe=a(norm_pe), w_pe_gate=a(w_pe_gate),
                  w_pe_proj=a(w_pe_proj))
    B = x.shape[0]
    starts = [0, 1792, 3584, 5376]
    owned = [(0, 2304), (2304, 4096), (4096, 5888), (5888, 8192)]
    xw, pw, meta = [], [], []
    for b in range(B):
        for ci in range(4):
            w0 = starts[ci]
            xw.append(x[b, w0:w0 + WIN])
            pw.append(p[:, b, w0:w0 + WIN])
            meta.append((b, owned[ci][0], owned[ci][1], w0))
    outs = run_windows(xw, pw, params, NLAYER)
    out = np.empty_like(x)
    for (b, o0, o1, w0), ow in zip(meta, outs):
        out[b, o0:o1] = ow[o0 - w0:o1 - w0]
    return out
```
